# Optimizing a Trainium2 kernel written in Bass

```python
import math
import jax, jax.numpy as jnp
from jax import lax
import numpy as np

D_MODEL = 1024
BATCH = 8
SEQ = 4096
DEPTH = 2

SSD_EXPAND = 2
SSD_D_INNER = SSD_EXPAND * D_MODEL
SSD_HEAD_DIM = 64
SSD_N_HEADS = SSD_D_INNER // SSD_HEAD_DIM
SSD_N_GROUPS = 8
SSD_HEADS_PER_GROUP = SSD_N_HEADS // SSD_N_GROUPS
SSD_D_STATE = 128
SSD_CONV_WIDTH = 4
SSD_CHUNK = 128
SSD_CONV_DIM = SSD_D_INNER + 2 * SSD_N_GROUPS * SSD_D_STATE
SSD_IN_DIM = SSD_D_INNER + SSD_CONV_DIM + SSD_N_HEADS

ATTN_HEAD_DIM = 64
ATTN_N_Q_HEADS = D_MODEL // ATTN_HEAD_DIM
ATTN_N_KV_HEADS = 4
ATTN_REP = ATTN_N_Q_HEADS // ATTN_N_KV_HEADS
ATTN_WINDOW = 128
ATTN_QKV_DIM = (ATTN_N_Q_HEADS + 2 * ATTN_N_KV_HEADS) * ATTN_HEAD_DIM

D_FF = 4 * D_MODEL

N_MIXERS = 2
N_SSD_LAYERS = (DEPTH + 1) // 2
N_ATTN_LAYERS = DEPTH // 2
NORM_EPS = 1e-6

kernel_name = "hybrid_ssd_swa_sink_sqrelu_trunk"


def rms_norm(x, w):
    xf = x.astype(jnp.float32)
    y = xf * lax.rsqrt(jnp.mean(xf * xf, axis=-1, keepdims=True) + NORM_EPS)
    return (y * w.astype(jnp.float32)).astype(x.dtype)


def causal_depthwise_conv(x, w, b):
    c = x.shape[-1]
    y = lax.conv_general_dilated(
        x, w.astype(x.dtype)[:, None, :], window_strides=(1,),
        padding=[(SSD_CONV_WIDTH - 1, 0)],
        dimension_numbers=("NWC", "WIO", "NWC"), feature_group_count=c)
    return y + b.astype(x.dtype)


def ssd_chunked_scan(xs, dt, a, bmat, cmat):
    b, l, g, j, p = xs.shape
    n = bmat.shape[-1]
    nc = l // SSD_CHUNK

    def to_chunks(t):
        t = t.astype(jnp.float32).reshape((b, nc, SSD_CHUNK) + t.shape[2:])
        return jnp.moveaxis(t, 1, 0)

    xc, dtc, bc, cc = to_chunks(xs), to_chunks(dt), to_chunks(bmat), to_chunks(cmat)
    ac = dtc * a.astype(jnp.float32)
    causal = jnp.tril(jnp.ones((SSD_CHUNK, SSD_CHUNK), dtype=bool))[None, :, :, None, None]

    def step(state, inp):
        x_q, dt_q, a_q, b_q, c_q = inp
        cum = jnp.cumsum(a_q, axis=1)
        diff = cum[:, :, None] - cum[:, None, :]
        decay = jnp.exp(jnp.where(causal, diff, -jnp.inf))
        cb = jnp.einsum("btgn,bsgn->btsg", c_q, b_q)
        y_intra = jnp.einsum("btsg,btsgj,bsgj,bsgjp->btgjp", cb, decay, dt_q, x_q)
        y_inter = jnp.einsum("btgn,bgjpn->btgjp", c_q, state) * jnp.exp(cum)[..., None]
        decay_to_end = jnp.exp(cum[:, -1:] - cum)
        new_state = state * jnp.exp(cum[:, -1])[..., None, None] + jnp.einsum(
            "bsgn,bsgj,bsgjp->bgjpn", b_q, dt_q * decay_to_end, x_q)
        return new_state, y_intra + y_inter

    state0 = jnp.zeros((b, g, j, p, n), jnp.float32)
    _, ys = lax.scan(step, state0, (xc, dtc, ac, bc, cc))
    return jnp.moveaxis(ys, 0, 1).reshape(b, l, g, j, p)


def ssd_mixer(u, w_in, conv_w, conv_b, dt_bias, a_log, d_skip, norm_w, w_out):
    b, l, _ = u.shape
    zxbcdt = u @ w_in
    z = zxbcdt[..., :SSD_D_INNER]
    xbc = zxbcdt[..., SSD_D_INNER:SSD_D_INNER + SSD_CONV_DIM]
    dt_raw = zxbcdt[..., SSD_D_INNER + SSD_CONV_DIM:]
    xbc = jax.nn.silu(causal_depthwise_conv(xbc, conv_w, conv_b))
    gn = SSD_N_GROUPS * SSD_D_STATE
    xs = xbc[..., :SSD_D_INNER].reshape(b, l, SSD_N_GROUPS, SSD_HEADS_PER_GROUP, SSD_HEAD_DIM)
    bmat = xbc[..., SSD_D_INNER:SSD_D_INNER + gn].reshape(b, l, SSD_N_GROUPS, SSD_D_STATE)
    cmat = xbc[..., SSD_D_INNER + gn:].reshape(b, l, SSD_N_GROUPS, SSD_D_STATE)
    dt = jax.nn.softplus(dt_raw.astype(jnp.float32) + dt_bias.astype(jnp.float32))
    dt = dt.reshape(b, l, SSD_N_GROUPS, SSD_HEADS_PER_GROUP)
    a = -jnp.exp(a_log.astype(jnp.float32)).reshape(SSD_N_GROUPS, SSD_HEADS_PER_GROUP)
    y = ssd_chunked_scan(xs, dt, a, bmat, cmat)
    y = y + d_skip.astype(jnp.float32).reshape(SSD_N_GROUPS, SSD_HEADS_PER_GROUP, 1) * xs.astype(jnp.float32)
    y = y.reshape(b, l, SSD_D_INNER) * jax.nn.silu(z.astype(jnp.float32))
    y = y.reshape(b, l, SSD_N_GROUPS, SSD_D_INNER // SSD_N_GROUPS)
    y = y * lax.rsqrt(jnp.mean(y * y, axis=-1, keepdims=True) + NORM_EPS)
    y = (y.reshape(b, l, SSD_D_INNER) * norm_w.astype(jnp.float32)).astype(u.dtype)
    return y @ w_out


def swa_sink_attention(u, w_qkv, b_qkv, sinks, w_o, b_o):
    b, l, _ = u.shape
    nb = l // ATTN_WINDOW
    qkv = u @ w_qkv + b_qkv
    qd = ATTN_N_Q_HEADS * ATTN_HEAD_DIM
    kd = ATTN_N_KV_HEADS * ATTN_HEAD_DIM
    q = qkv[..., :qd].reshape(b, nb, ATTN_WINDOW, ATTN_N_KV_HEADS, ATTN_REP, ATTN_HEAD_DIM)
    k = qkv[..., qd:qd + kd].reshape(b, nb, ATTN_WINDOW, ATTN_N_KV_HEADS, ATTN_HEAD_DIM)
    v = qkv[..., qd + kd:].reshape(b, nb, ATTN_WINDOW, ATTN_N_KV_HEADS, ATTN_HEAD_DIM)

    def with_prev_block(t):
        prev = jnp.concatenate([jnp.zeros_like(t[:, :1]), t[:, :-1]], axis=1)
        return jnp.concatenate([prev, t], axis=2)

    kb, vb = with_prev_block(k), with_prev_block(v)
    scores = jnp.einsum("bnqkrd,bnskd->bnkrqs", q, kb).astype(jnp.float32) * (ATTN_HEAD_DIM ** -0.5)
    qpos = jnp.arange(ATTN_WINDOW) + ATTN_WINDOW
    kpos = jnp.arange(2 * ATTN_WINDOW)
    rel = qpos[:, None] - kpos[None, :]
    band = (rel >= 0) & (rel < ATTN_WINDOW)
    blk = jnp.arange(nb)[:, None, None]
    valid = band[None] & ~((blk == 0) & (kpos[None, None, :] < ATTN_WINDOW))
    scores = jnp.where(valid[None, :, None, None], scores, -jnp.inf)
    sink = sinks.astype(jnp.float32).reshape(1, 1, ATTN_N_KV_HEADS, ATTN_REP, 1, 1)
    m = jnp.maximum(jnp.max(scores, axis=-1, keepdims=True), sink)
    e = jnp.exp(scores - m)
    probs = e / (jnp.sum(e, axis=-1, keepdims=True) + jnp.exp(sink - m))
    out = jnp.einsum("bnkrqs,bnskd->bnqkrd", probs.astype(vb.dtype), vb).reshape(b, l, qd)
    return out @ w_o + b_o


def sqrelu_mlp(u, w_up, w_down):
    return jnp.square(jax.nn.relu(u @ w_up)) @ w_down


def setup_inputs(seed: int = 0) -> dict:
    key = jax.random.key(seed)
    ks = jax.random.split(key, 24)
    f32 = jnp.float32

    def normal(k, shape, scale):
        return jax.random.normal(k, shape, f32) * scale

    def gain(k, shape):
        return 1.0 + 0.05 * jax.random.normal(k, shape, f32)

    x = jax.random.normal(ks[0], (BATCH, SEQ, D_MODEL), f32)
    ssd_w_in = normal(ks[1], (N_SSD_LAYERS, D_MODEL, SSD_IN_DIM), D_MODEL ** -0.5)
    ssd_conv_w = normal(ks[2], (N_SSD_LAYERS, SSD_CONV_WIDTH, SSD_CONV_DIM), SSD_CONV_WIDTH ** -0.5)
    ssd_conv_b = normal(ks[3], (N_SSD_LAYERS, SSD_CONV_DIM), 0.02)
    dt0 = jnp.exp(jax.random.uniform(ks[4], (N_SSD_LAYERS, SSD_N_HEADS), f32,
                                     math.log(1e-3), math.log(1e-1)))
    ssd_dt_bias = dt0 + jnp.log(-jnp.expm1(-dt0))
    ssd_a_log = jnp.log(jax.random.uniform(ks[5], (N_SSD_LAYERS, SSD_N_HEADS), f32, 1.0, 16.0))
    ssd_d = gain(ks[6], (N_SSD_LAYERS, SSD_N_HEADS))
    ssd_norm_w = gain(ks[7], (N_SSD_LAYERS, SSD_D_INNER))
    ssd_w_out = normal(ks[8], (N_SSD_LAYERS, SSD_D_INNER, D_MODEL), SSD_D_INNER ** -0.5)
    attn_w_qkv = normal(ks[9], (N_ATTN_LAYERS, D_MODEL, ATTN_QKV_DIM), D_MODEL ** -0.5)
    attn_b_qkv = normal(ks[10], (N_ATTN_LAYERS, ATTN_QKV_DIM), 0.02)
    attn_sinks = normal(ks[11], (N_ATTN_LAYERS, ATTN_N_Q_HEADS), 1.0)
    attn_w_o = normal(ks[12], (N_ATTN_LAYERS, ATTN_N_Q_HEADS * ATTN_HEAD_DIM, D_MODEL),
                      (ATTN_N_Q_HEADS * ATTN_HEAD_DIM) ** -0.5)
    attn_b_o = normal(ks[13], (N_ATTN_LAYERS, D_MODEL), 0.02)
    mlp_w_up = normal(ks[14], (DEPTH, D_MODEL, D_FF), D_MODEL ** -0.5)
    mlp_w_down = normal(ks[15], (DEPTH, D_FF, D_MODEL), D_FF ** -0.5)
    mix_pre_norm = gain(ks[16], (DEPTH, D_MODEL))
    mix_post_norm = gain(ks[17], (DEPTH, D_MODEL))
    ffn_pre_norm = gain(ks[18], (DEPTH, D_MODEL))
    ffn_post_norm = gain(ks[19], (DEPTH, D_MODEL))
    return {
        "x": x,
        "ssd_w_in": ssd_w_in, "ssd_conv_w": ssd_conv_w, "ssd_conv_b": ssd_conv_b,
        "ssd_dt_bias": ssd_dt_bias, "ssd_a_log": ssd_a_log, "ssd_d": ssd_d,
        "ssd_norm_w": ssd_norm_w, "ssd_w_out": ssd_w_out,
        "attn_w_qkv": attn_w_qkv, "attn_b_qkv": attn_b_qkv, "attn_sinks": attn_sinks,
        "attn_w_o": attn_w_o, "attn_b_o": attn_b_o,
        "mlp_w_up": mlp_w_up, "mlp_w_down": mlp_w_down,
        "mix_pre_norm": mix_pre_norm, "mix_post_norm": mix_post_norm,
        "ffn_pre_norm": ffn_pre_norm, "ffn_post_norm": ffn_post_norm,
    }


def reference(x, ssd_w_in, ssd_conv_w, ssd_conv_b, ssd_dt_bias, ssd_a_log, ssd_d,
              ssd_norm_w, ssd_w_out, attn_w_qkv, attn_b_qkv, attn_sinks, attn_w_o,
              attn_b_o, mlp_w_up, mlp_w_down, mix_pre_norm, mix_post_norm,
              ffn_pre_norm, ffn_post_norm):
    h = x
    for i in range(DEPTH):
        u = rms_norm(h, mix_pre_norm[i])
        j = i // N_MIXERS
        if i % N_MIXERS == 0:
            mix = ssd_mixer(u, ssd_w_in[j], ssd_conv_w[j], ssd_conv_b[j], ssd_dt_bias[j],
                            ssd_a_log[j], ssd_d[j], ssd_norm_w[j], ssd_w_out[j])
        else:
            mix = swa_sink_attention(u, attn_w_qkv[j], attn_b_qkv[j], attn_sinks[j],
                                     attn_w_o[j], attn_b_o[j])
        h = h + rms_norm(mix, mix_post_norm[i])
        f = sqrelu_mlp(rms_norm(h, ffn_pre_norm[i]), mlp_w_up[i], mlp_w_down[i])
        h = h + rms_norm(f, ffn_post_norm[i])
    return h
```

```python
import numpy as np
from contextlib import ExitStack
import concourse.bass as bass
import concourse.mybir as mybir
from concourse.bass_utils import run_bass_kernel_spmd

F32, BF16 = mybir.dt.float32, mybir.dt.bfloat16
AF = mybir.ActivationFunctionType
ALU = mybir.AluOpType
AX = mybir.AxisListType

T = 512
NCH = 4
L = 4096
NT = L // T
DM = 1024
EPS = 1e-6
NEG = -30000.0
WBLK = 4096
NSLOT = 4
NBLK0 = 32
NBLK1 = 22

PC_NW = 0
PC_CONVB = 64
PC_CONVW = 96
PC_QB = 224
PC_KB = 232
PC_OB = 236
PC_DTB = 244
PC_ALOG = 245
PC_SNW = 246
NPC = 262
PR_D = 0
PR_BV = 32
PR_SINK = 288
NPR = 304
C_ID = 0
C_RESET = 128
NCST = 640


class Region:
    def __init__(self, name):
        self.name = name
        self.recs = []


class Buf:
    __slots__ = ("reg", "lo", "hi")

    def __init__(self, reg, lo, hi):
        self.reg, self.lo, self.hi = reg, lo, hi


class Eng:
    def __init__(self, name, ordered):
        self.name = name
        self.q = []
        self.seen = {}
        self.ordered = ordered
        self.sem = None


class Ins:
    __slots__ = ("fn", "waits", "needed", "dsem")

    def __init__(self, fn, waits, dsem=None):
        self.fn, self.waits, self.needed, self.dsem = fn, waits, False, dsem


class X:
    def __init__(self, ap, reg, base, esz, fshape):
        self.ap, self.reg, self.base, self.esz, self.fshape = ap, reg, base, esz, list(fshape)
        st = 1
        for d in fshape[1:]:
            st *= d
        self.st0 = st * esz
        self.nbytes = st * fshape[0] * esz

    def b(self, i0=None, i1=None):
        if i0 is None:
            return Buf(self.reg, self.base, self.base + self.nbytes)
        if i1 is None:
            i1 = i0 + 1
        return Buf(self.reg, self.base + i0 * self.st0, self.base + i1 * self.st0)


class Sched:
    def __init__(self):
        self.PE = Eng("pe", True)
        self.ACT = Eng("act", False)
        self.DVE = Eng("dve", False)
        self.POOL = Eng("pool", False)
        self.SP = Eng("sp", True)
        self.engs = [self.PE, self.ACT, self.DVE, self.POOL, self.SP]

    def _deps(self, E, reads, writes, tok):
        deps = []
        for is_w, lst in ((False, reads), (True, writes)):
            for bf in lst:
                lo, hi = bf.lo, bf.hi
                for r in bf.reg.recs:
                    rlo, rhi, rw, rt = r
                    if rhi <= lo or rlo >= hi:
                        continue
                    if is_w or rw:
                        deps.append(rt)
        for is_w, lst in ((False, reads), (True, writes)):
            for bf in lst:
                reg, lo, hi = bf.reg, bf.lo, bf.hi
                new = []
                for r in reg.recs:
                    rlo, rhi, rw, rt = r
                    if rhi <= lo or rlo >= hi:
                        new.append(r)
                        continue
                    cov = lo <= rlo and rhi <= hi
                    if rt is tok:
                        if not cov:
                            new.append(r)
                        continue
                    if is_w and cov:
                        continue
                    if (not is_w) and (not rw) and cov and rt[0] == 'e' and tok[0] == 'e' and rt[1] is tok[1]:
                        continue
                    new.append(r)
                new.append((lo, hi, is_w, tok))
                reg.recs = new
        return deps

    def _waits(self, E, deps):
        waits = {}
        for d in deps:
            if d[0] == 'e':
                _, E2, idx = d
                if E2 is E and E.ordered:
                    continue
                key = E2.name
                if E.seen.get(key, -1) >= idx:
                    continue
                if key not in waits or waits[key][2] < idx:
                    waits[key] = d
            else:
                _, sem, val, key = d
                if E.seen.get(key, -1) >= val:
                    continue
                if key not in waits or waits[key][2] < val:
                    waits[key] = d
        out = []
        for key, d in waits.items():
            E.seen[key] = d[2]
            if d[0] == 'e':
                d[1].q[d[2]].needed = True
            out.append(d)
        return out

    def op(self, E, fn, r=(), w=()):
        idx = len(E.q)
        tok = ('e', E, idx)
        deps = self._deps(E, r, w, tok)
        E.q.append(Ins(fn, self._waits(E, deps)))
        return tok

    def dma(self, E, fn, dsem, r=(), w=()):
        dsem['val'] += 16
        tok = ('d', dsem['sem'], dsem['val'], dsem['key'])
        deps = self._deps(E, r, w, tok)
        E.q.append(Ins(fn, self._waits(E, deps), dsem=dsem['sem']))
        return tok

    def wait_tokens(self, E, toks):
        E.q.append(Ins(None, self._waits(E, toks)))

    def replay(self, nc, block):
        pref = {}
        for E in self.engs:
            c = 0
            p = []
            for ins in E.q:
                if ins.needed:
                    c += 1
                p.append(c)
            pref[E.name] = p

        def run(E, e):
            fuse = E.name in ("act", "dve")
            for ins in E.q:
                ws = []
                for d in ins.waits:
                    if d[0] == 'e':
                        ws.append((d[1].sem, pref[d[1].name][d[2]]))
                    else:
                        ws.append((d[1], d[2]))
                last = None
                if fuse and ins.fn is not None and ws:
                    last = ws.pop()
                for sm, v in ws:
                    e.wait_ge(sm, v)
                if ins.fn is None:
                    continue
                bi = ins.fn(e)
                if last is not None:
                    bi._wait_ge(last[0], last[1])
                if ins.dsem is not None:
                    bi.then_inc(ins.dsem, 16)
                elif ins.needed:
                    bi.then_inc(E.sem, 1)

        S = self

        @block.tensor
        def _(e):
            run(S.PE, e)

        @block.scalar
        def _(e):
            run(S.ACT, e)

        @block.vector
        def _(e):
            run(S.DVE, e)

        @block.gpsimd
        def _(e):
            run(S.POOL, e)

        @block.sync
        def _(e):
            run(S.SP, e)


class _Stop(Exception):
    pass


def build(layers=(0, 1), nt=NT, taps=None, stop=None):
    taps = taps or {}

    def chk(n):
        if stop == n:
            raise _Stop()
    nc = bass.Bass("TRN2", target_bir_lowering=False)
    x_d = nc.dram_tensor("x", [L, DM], F32, kind="ExternalInput").ap()
    o_d = nc.dram_tensor("out", [L, DM], F32, kind="ExternalOutput").ap()
    ws_d = nc.dram_tensor("wstream", [NBLK0 + NBLK1, 128, WBLK], F32, kind="ExternalInput").ap()
    wdt_d = nc.dram_tensor("wdt", [128, 512], F32, kind="ExternalInput").ap()
    pc_d = nc.dram_tensor("pcols", [128, NPC], F32, kind="ExternalInput").ap()
    pr_d = nc.dram_tensor("prep", [128, NPR], F32, kind="ExternalInput").ap()
    cst_d = nc.dram_tensor("cst", [128, NCST], F32, kind="ExternalInput").ap()
    msk_d = nc.dram_tensor("msk", [128, 1024], F32, kind="ExternalInput").ap()
    tap_d = {}
    for name, shp in taps.items():
        tap_d[name] = nc.dram_tensor("tap_" + name, list(shp), F32, kind="ExternalOutput").ap()

    sc = Sched()
    PE, ACT, DVE, POOL, SP = sc.PE, sc.ACT, sc.DVE, sc.POOL, sc.SP

    with ExitStack() as es:
        def sb(name, fshape, dt):
            t = es.enter_context(nc.sbuf_tensor("sb_" + name, [128] + list(fshape), dt))
            esz = 4 if dt == F32 else 2
            return X(t[:], Region(name), 0, esz, fshape)

        def sem(name):
            return es.enter_context(nc.semaphore(name))

        for E in sc.engs:
            E.sem = sem("s_" + E.name)

        def dsem(name):
            return {'sem': sem(name), 'val': 0, 'key': name}

        hT = sb("hT", [8, T], F32)
        uT = sb("uT", [8, T], BF16)
        mT = sb("mT", [8, T], F32)
        xin = [sb("xin%d" % i, [DM], F32) for i in range(2)]
        xout = [sb("xout%d" % i, [DM], F32) for i in range(2)]
        wsl = [sb("wsl%d" % i, [WBLK], BF16) for i in range(NSLOT)]
        sq = [sb("sq%d" % i, [T], BF16) for i in range(2)]
        rstd = sb("rstd", [T], F32)
        lnt = rstd
        cst = sb("cst", [NCST], F32)
        ident_b = sb("ident_b", [128], BF16)
        ones_b = sb("ones_b", [128], BF16)
        ones_f = sb("ones_f", [128], F32)
        mscan_b = sb("mscan_b", [512], BF16)
        matt_b = sb("matt_b", [512], BF16)
        pcols = sb("pcols", [NPC], F32)
        prep = sb("prep", [NPR], F32)
        nsink = sb("nsink", [16], F32)
        nA = sb("nA", [1], F32)
        wdt = sb("wdt", [8, 64], BF16)
        S = sb("S", [8, 256], F32)
        S_bf = sb("S_bf", [8, 256], BF16)
        tails = sb("tails", [32, 3], BF16)
        dc = sb("dc", [T], F32)
        atmp = sb("atmp", [T], F32)
        csp = sb("csp", [3, T], BF16)
        cres = sb("cres", [T], F32)
        tk = [sb("tk%d" % c, [5, 32], F32) for c in range(NCH)]
        tkt = sb("tkt", [32], F32)
        diagc = sb("diagc", [NCH, 32], F32)
        clS = sb("clS", [128], F32)
        kprev = sb("kprev", [4, 128], BF16)
        vprev = sb("vprev", [256], BF16)
        ssg = sb("ssg", [2, 4], F32)
        rsg = sb("rsg", [2, 4], F32)
        junk = sb("junk", [256], BF16)
        mr = sb("mr", [2, 16], F32)
        negm = sb("negm", [2, 16], F32)
        rsum = sb("rsum", [2, 16], F32)
        esk = sb("esk", [2, 16], F32)
        rden = sb("rden", [2, 16], F32)

        ARENA = 78 * 1024
        arena_t = es.enter_context(nc.sbuf_tensor("arena", [128, ARENA // 2], BF16))
        arena_reg = Region("arena")
        aoff = [0]

        def carve(fshape, dt, at=None):
            esz = 4 if dt == F32 else 2
            n = 1
            for d in fshape:
                n *= d
            nb = n * esz
            if at is None:
                at = aoff[0]
                aoff[0] += nb
            assert at % 4 == 0 and at + nb <= ARENA, (at, nb)
            ap = arena_t[:, at // 2:(at + nb) // 2]
            if dt == F32:
                ap = ap.bitcast(F32)
            if len(fshape) == 2:
                ap = ap.rearrange("p (a b) -> p a b", a=fshape[0])
            elif len(fshape) == 3:
                ap = ap.rearrange("p (a b c) -> p a b c", a=fshape[0], b=fshape[1])
            return X(ap, arena_reg, at, esz, fshape)

        aoff[0] = 0
        zs = [carve([NCH, 512], BF16) for _ in range(2)]
        xraw = [carve([4, T + 8], BF16) for _ in range(2)]
        xc = [carve([4, T], BF16) for _ in range(2)]
        xbt = [carve([NCH, 384], BF16) for _ in range(2)]
        xw = [carve([NCH, 256], BF16) for _ in range(2)]
        dg = carve([16, 128], BF16)
        DIg = [carve([4, 128], BF16) for _ in range(2)]
        Ebuf = [carve([128], F32) for _ in range(4)]
        MTb = [carve([16, 128], BF16) for _ in range(2)]
        yi = [carve([256], F32) for _ in range(2)]
        t2 = [carve([256], F32) for _ in range(2)]
        yg = [carve([NCH, 256], BF16) for _ in range(2)]
        yn = carve([NCH, 256], BF16)
        ynT = carve([16, T], BF16)
        ssd_end = aoff[0]
        aoff[0] = 0
        aT = carve([32, T], BF16)
        rl = [carve([T], BF16) for _ in range(2)]
        mlp_end = aoff[0]
        aoff[0] = 0
        qT = carve([8, T], BF16)
        kT = carve([4, 128 + T], BF16)
        Vt = carve([NCH + 1, 256], BF16)
        P4 = [carve([4, 256], BF16) for _ in range(2)]
        PT4 = [carve([1024], BF16) for _ in range(2)]
        ao = [carve([1024], BF16) for _ in range(2)]
        aoT = carve([8, T], BF16)
        att_end = aoff[0]
        assert max(ssd_end, mlp_end, att_end) <= ARENA

        ps = []
        for i in range(8):
            t = es.enter_context(nc.psum_tensor("ps%d" % i, [128, 512], F32))
            ps.append(X(t[:], Region("ps%d" % i), 0, 4, [512]))
        rot = {'A': [0, 1, 2, 3], 'B': [4, 5], 'C': [6, 7]}
        roti = {'A': 0, 'B': 0, 'C': 0}

        def bank(pool='A'):
            i = rot[pool][roti[pool] % len(rot[pool])]
            roti[pool] += 1
            return ps[i]

        ds_w = [dsem("dw%d" % i) for i in range(NSLOT)]
        ds_xin = [dsem("dxi%d" % i) for i in range(2)]
        ds_xout = [dsem("dxo%d" % i) for i in range(2)]
        ds_c = dsem("dcst")
        ds_tap = dsem("dtap")

        def mm(out, lhsT, rhs, start=True, stop=True, r=(), w=()):
            return sc.op(PE, lambda e: e.matmul(out, lhsT, rhs, start=start, stop=stop), r, w)

        def trp(out, in_, ident, r=(), w=()):
            return sc.op(PE, lambda e: e.transpose(out, in_, ident), r, w)

        def act(out, in_, func, bias=None, scale=None, accum=None, r=(), w=()):
            kw = {}
            if bias is not None:
                kw['bias'] = bias
            if scale is not None:
                kw['scale'] = scale
            if accum is not None:
                kw['accum_out'] = accum
            return sc.op(ACT, lambda e: e.activation(out, in_, func, **kw), r, w)

        def vtt(out, in0, in1, op_, r=(), w=(), E=None):
            return sc.op(E or DVE, lambda e: e.tensor_tensor(out, in0, in1, op_), r, w)

        def vts(out, in0, s1, s2, op0, op1=None, r=(), w=(), E=None):
            if op1 is None:
                return sc.op(E or DVE, lambda e: e.tensor_scalar(out, in0, s1, None, op0), r, w)
            return sc.op(E or DVE, lambda e: e.tensor_scalar(out, in0, s1, s2, op0, op1), r, w)

        def vstt(out, in0, scalar, in1, op0, op1, r=(), w=()):
            return sc.op(DVE, lambda e: e.scalar_tensor_tensor(out, in0, scalar, in1, op0, op1), r, w)

        def vcopy(out, in_, r=(), w=(), E=None):
            return sc.op(E or DVE, lambda e: e.tensor_copy(out, in_), r, w)

        def vmemset(ap, val, w=(), E=None):
            return sc.op(E or DVE, lambda e: e.memset(ap, val), (), w)

        out_tokens = []

        def tap(name, src_ap, rb):
            if name in tap_d:
                tokn = sc.dma(POOL, lambda e: e.dma_start(out=tap_d[name], in_=src_ap), dsem("dtap_" + name), r=[rb])
                out_tokens.append(tokn)

        def ld(dst, src_ap):
            sc.dma(SP, lambda e: e.dma_start(out=dst.ap, in_=src_ap), dsem("dld_" + dst.reg.name), w=[dst.b()])

        ld(cst, cst_d[:, :])
        ld(pcols, pc_d[:, :])
        ld(prep, pr_d[:, :])
        sc.dma(POOL, lambda e: e.dma_start(out=wdt.ap.rearrange("p a b -> p (a b)"), in_=wdt_d[:, :]),
               dsem("dld_wdt"), w=[wdt.b()])
        vcopy(ident_b.ap, cst.ap[:, C_ID:C_ID + 128], r=[cst.b()], w=[ident_b.b()])
        sc.dma(POOL, lambda e: e.dma_start(out=mscan_b.ap, in_=msk_d[:, 0:512]), dsem("dld_mscan"), w=[mscan_b.b()])
        sc.dma(POOL, lambda e: e.dma_start(out=matt_b.ap, in_=msk_d[:, 512:1024]), dsem("dld_matt"), w=[matt_b.b()])
        vmemset(ones_b.ap, 1.0, w=[ones_b.b()])
        vmemset(ones_f.ap, 1.0, w=[ones_f.b()])
        vmemset(S.ap, 0.0, w=[S.b()])
        vmemset(S_bf.ap, 0.0, w=[S_bf.b()])
        vmemset(tails.ap, 0.0, w=[tails.b()])
        vmemset(kprev.ap, 0.0, w=[kprev.b()])
        vmemset(vprev.ap, 0.0, w=[vprev.b()])
        vmemset(atmp.ap, 0.0, w=[atmp.b()])
        vmemset(dc.ap, 0.0, w=[dc.b()])
        vmemset(diagc.ap, 0.0, w=[diagc.b()])
        vts(nsink.ap, prep.ap[:, PR_SINK:PR_SINK + 16], -1.0, None, ALU.mult, r=[prep.b()], w=[nsink.b()])
        act(nA.ap[0:64, :], pcols.ap[0:64, PC_ALOG:PC_ALOG + 1], AF.Exp, r=[pcols.b()], w=[nA.b()])
        vts(nA.ap[0:64, :], nA.ap[0:64, :], -1.0, None, ALU.mult, r=[nA.b()], w=[nA.b()])

        wstate = {'next': 0, 'total': 0}
        blk_list = []
        for ti in range(nt):
            if 0 in layers:
                blk_list += list(range(0, NBLK0))
            if 1 in layers:
                blk_list += list(range(NBLK0, NBLK0 + NBLK1))
        wstate['total'] = len(blk_list)

        def w_issue():
            i = wstate['next']
            if i >= wstate['total']:
                return
            wstate['next'] += 1
            slot = i % NSLOT
            src = ws_d[blk_list[i]]
            dst = wsl[slot]
            sc.dma(POOL, lambda e: e.dma_start(out=dst.ap, in_=src), ds_w[slot], w=[dst.b()])

        wuse = {'i': 0}

        def w_next():
            i = wuse['i']
            wuse['i'] += 1
            while wstate['next'] < min(i + NSLOT, wstate['total']):
                w_issue()
            return wsl[i % NSLOT]

        def norm_stats(src_chunks, nfeat):
            sb_ = bank('A')
            n = len(src_chunks)
            for i, (ap, bf) in enumerate(src_chunks):
                s_ = sq[i % 2]
                act(s_.ap, ap, AF.Square, r=[bf], w=[s_.b()])
                mm(sb_.ap, ones_b.ap, s_.ap, start=(i == 0), stop=(i == n - 1),
                   r=[ones_b.b(), s_.b()], w=[sb_.b()])
            finish_stats(sb_, nfeat)

        def finish_stats(sb_, nfeat):
            act(lnt.ap, sb_.ap, AF.Ln, bias=EPS, scale=1.0 / nfeat, r=[sb_.b()], w=[lnt.b()])
            act(rstd.ap, lnt.ap, AF.Exp, scale=-0.5, r=[lnt.b()], w=[rstd.b()])

        def pre_norm(n):
            norm_stats([(hT.ap[:, f, :], hT.b(f)) for f in range(8)], DM)
            for f in range(8):
                vstt(uT.ap[:, f, :], hT.ap[:, f, :], pcols.ap[:, PC_NW + n * 8 + f:PC_NW + n * 8 + f + 1],
                     rstd.ap, ALU.mult, ALU.mult, r=[hT.b(f), pcols.b(), rstd.b()], w=[uT.b(f)])

        def post_norm(n, dst=None):
            dst = dst or hT
            for f in range(8):
                vtt(mT.ap[:, f, :], mT.ap[:, f, :], rstd.ap, ALU.mult, r=[mT.b(f), rstd.b()], w=[mT.b(f)])
                vstt(dst.ap[:, f, :], mT.ap[:, f, :], pcols.ap[:, PC_NW + n * 8 + f:PC_NW + n * 8 + f + 1],
                     hT.ap[:, f, :], ALU.mult, ALU.add, r=[mT.b(f), pcols.b(), hT.b(f)], w=[dst.b(f)])

        class OutAcc:
            def __init__(self, bias_col=None):
                self.sb_ = bank('B')
                self.pend = None
                self.i = 0
                self.bias_col = bias_col

            def chunk(self, oc, bk):
                if self.bias_col is None:
                    act(mT.ap[:, oc, :], bk.ap, AF.Copy, r=[bk.b()], w=[mT.b(oc)])
                else:
                    c0 = self.bias_col + oc
                    act(mT.ap[:, oc, :], bk.ap, AF.Identity, bias=pcols.ap[:, c0:c0 + 1],
                        r=[bk.b(), pcols.b()], w=[mT.b(oc)])
                s_ = sq[oc % 2]
                act(s_.ap, mT.ap[:, oc, :], AF.Square, r=[mT.b(oc)], w=[s_.b()])
                self.flush()
                self.pend = (oc, s_)

            def flush(self, last=False):
                if self.pend is not None:
                    oc, s_ = self.pend
                    mm(self.sb_.ap, ones_b.ap, s_.ap, start=(self.i == 0), stop=last,
                       r=[ones_b.b(), s_.b()], w=[self.sb_.b()])
                    self.i += 1
                    self.pend = None

            def finish(self):
                self.flush(last=True)
                finish_stats(self.sb_, DM)

        def mlp(n_pre, n_post, dst=None):
            pre_norm(n_pre)
            for b_ in range(8):
                sl = w_next()
                w3 = sl.ap.rearrange("p (k j) -> p k j", k=8)
                for j in range(4):
                    fc = 4 * b_ + j
                    bk = bank('A')
                    for k in range(8):
                        mm(bk.ap, w3[:, k, j * 128:(j + 1) * 128], uT.ap[:, k, :], start=(k == 0), stop=(k == 7),
                           r=[sl.b(), uT.b(k)], w=[bk.b()])
                    r_ = rl[fc % 2]
                    act(r_.ap, bk.ap, AF.Relu, r=[bk.b()], w=[r_.b()])
                    vtt(aT.ap[:, fc, :], r_.ap, bk.ap, ALU.mult, r=[r_.b(), bk.b()], w=[aT.b(fc)])
            oa = OutAcc()
            for oc in range(8):
                sl = w_next()
                w3 = sl.ap.rearrange("p (k j) -> p k j", k=32)
                bk = bank('A')
                for k in range(32):
                    mm(bk.ap, w3[:, k, :], aT.ap[:, k, :], start=(k == 0), stop=(k == 31),
                       r=[sl.b(), aT.b(k)], w=[bk.b()])
                oa.chunk(oc, bk)
            oa.finish()
            post_norm(n_post, dst)

        def ssd_layer(ti):
            pre_norm(0)
            chk(1)
            def ssd_A(g):
                par = g % 2
                for j in range(4):
                    col = PC_CONVW + (4 * g + j) * 4
                    vtt(dg.ap[:, 4 * j:4 * j + 4, :], ident_b.ap.unsqueeze(1).broadcast_to([128, 4, 128]),
                        pcols.ap[:, col:col + 4].unsqueeze(2).broadcast_to([128, 4, 128]), ALU.mult,
                        r=[ident_b.b(), pcols.b()], w=[dg.b(4 * j, 4 * j + 4)], E=POOL)
                DI = DIg[par]
                col = PR_D + 4 * g
                vtt(DI.ap, ident_b.ap.unsqueeze(1).broadcast_to([128, 4, 128]),
                    prep.ap[:, col:col + 4].unsqueeze(2).broadcast_to([128, 4, 128]), ALU.mult,
                    r=[ident_b.b(), prep.b()], w=[DI.b()], E=POOL)
                if g % 2 == 0:
                    sl = w_next()
                    w3 = sl.ap.rearrange("p (k j) -> p k j", k=8)
                    zc = zs[(g // 2) % 2]
                    for c in range(NCH):
                        bk = bank('A')
                        for k in range(8):
                            mm(bk.ap, uT.ap[:, k, c * 128:(c + 1) * 128], w3[:, k, :], start=(k == 0), stop=(k == 7),
                               r=[uT.b(k), sl.b()], w=[bk.b()])
                        act(zc.ap[:, c, :], bk.ap, AF.Silu, r=[bk.b()], w=[zc.b(c)])
                        yield
                sl = w_next()
                w3 = sl.ap.rearrange("p (k j) -> p k j", k=8)
                xr = xraw[par]
                xcg = xc[par]
                vcopy(xr.ap[:, :, 0:3], tails.ap[:, 4 * g:4 * g + 4, :], r=[tails.b(4 * g, 4 * g + 4)], w=[xr.b()])
                for j in range(4):
                    bk = bank('A')
                    for k in range(8):
                        mm(bk.ap, w3[:, k, j * 128:(j + 1) * 128], uT.ap[:, k, :], start=(k == 0), stop=(k == 7),
                           r=[sl.b(), uT.b(k)], w=[bk.b()])
                    if j == 2:
                        act(xr.ap[:, j, 3:3 + T], bk.ap, AF.Copy, r=[bk.b()], w=[xr.b(j)])
                    else:
                        vcopy(xr.ap[:, j, 3:3 + T], bk.ap, r=[bk.b()], w=[xr.b(j)])
                    yield
                vcopy(tails.ap[:, 4 * g:4 * g + 4, :], xr.ap[:, :, T:T + 3], r=[xr.b()], w=[tails.b(4 * g, 4 * g + 4)])
                yield
                for j in range(4):
                    bk = bank('A')
                    for k in range(4):
                        mm(bk.ap, dg.ap[:, 4 * j + k, :], xr.ap[:, j, k:k + T], start=(k == 0), stop=(k == 3),
                           r=[dg.b(4 * j + k), xr.b(j)], w=[bk.b()])
                    col = PC_CONVB + 4 * g + j
                    act(xcg.ap[:, j, :], bk.ap, AF.Silu, bias=pcols.ap[:, col:col + 1],
                        r=[bk.b(), pcols.b()], w=[xcg.b(j)])
                    yield
                if ti == 0 and g == 0:
                    tap("xc0", xcg.ap.rearrange("p a b -> p (a b)"), xcg.b())
                xbg = xbt[par]
                xwg = xw[par]
                for half in range(2):
                    tb = bank('A')
                    tbb = tb.ap.bitcast(BF16)
                    for cc in range(2):
                        c = 2 * half + cc
                        for j in range(3):
                            trp(tbb[:, cc * 384 + j * 128: cc * 384 + (j + 1) * 128], xcg.ap[:, j, c * 128:(c + 1) * 128],
                                ident_b.ap, r=[xcg.b(j), ident_b.b()], w=[tb.b()])
                    if half == 0:
                        act(xbg.ap[:, 2 * half:2 * half + 2, :], tbb[:, 0:768].rearrange("p (c f) -> p c f", c=2), AF.Copy,
                            r=[tb.b()], w=[xbg.b(2 * half, 2 * half + 2)])
                    else:
                        vcopy(xbg.ap[:, 2 * half:2 * half + 2, :], tbb[:, 0:768].rearrange("p (c f) -> p c f", c=2),
                              r=[tb.b()], w=[xbg.b(2 * half, 2 * half + 2)])
                    for cc in range(2):
                        c = 2 * half + cc
                        vtt(xwg.ap[:, c, :].rearrange("p (h d) -> p h d", h=4),
                            xbg.ap[:, c, 0:256].rearrange("p (h d) -> p h d", h=4),
                            tk[c].ap[:, 3, 4 * g:4 * g + 4].unsqueeze(2).broadcast_to([128, 4, 64]), ALU.mult,
                            r=[xbg.b(c), tk[c].b(3)], w=[xwg.b(c)], E=POOL)
                    yield
                Gb = bank('B')
                for c in range(NCH):
                    mm(Gb.ap[:, c * 128:(c + 1) * 128], xcg.ap[:, 2, c * 128:(c + 1) * 128],
                       xcg.ap[:, 3, c * 128:(c + 1) * 128], r=[xcg.b(2), xcg.b(3)], w=[Gb.b()])
                yield
                MT = MTb[par]
                for j in range(4):
                    h = 4 * g + j
                    Rb = bank('C')
                    for i in range(2):
                        mm(Rb.ap, ident_b.ap[32:64, 32 + h:33 + h].broadcast_to([32, 128]), csp.ap[32:64, i, :],
                           start=(i == 0), stop=False, r=[ident_b.b(), csp.b(i)], w=[Rb.b()])
                    mm(Rb.ap, ident_b.ap, mscan_b.ap, start=False, stop=True, r=[ident_b.b(), mscan_b.b()], w=[Rb.b()])
                    for c in range(NCH):
                        E_ = Ebuf[(j * NCH + c) % 4]
                        act(E_.ap, Rb.ap[:, c * 128:(c + 1) * 128], AF.Exp, bias=tk[c].ap[:, 1, h:h + 1],
                            r=[Rb.b(), tk[c].b(1)], w=[E_.b()])
                        vstt(MT.ap[:, j * NCH + c, :], Gb.ap[:, c * 128:(c + 1) * 128], tk[c].ap[:, 0, h:h + 1], E_.ap,
                             ALU.mult, ALU.mult, r=[Gb.b(), tk[c].b(0), E_.b()], w=[MT.b(j * NCH + c)])
                    yield
                if ti == 0 and g == 0:
                    tap("MT0", MT.ap.rearrange("p a b -> p (a b)"), MT.b())

            def ssd_B(g):
                par = g % 2
                zc = zs[(g // 2) % 2]
                zoff = (g % 2) * 256
                xcg, xbg, xwg, MT, DI, ygg = xc[par], xbt[par], xw[par], MTb[par], DIg[par], yg[par]
                for c in range(NCH):
                    yib = bank('A')
                    mm(yib.ap[:, 0:256], xcg.ap[:, 3, c * 128:(c + 1) * 128], S_bf.ap[:, g, :],
                       r=[xcg.b(3), S_bf.b(g)], w=[yib.b()])
                    yi_ = yi[c % 2]
                    vtt(yi_.ap.rearrange("p (h d) -> p h d", h=4), yib.ap[:, 0:256].rearrange("p (h d) -> p h d", h=4),
                        tk[c].ap[:, 2, 4 * g:4 * g + 4].unsqueeze(2).broadcast_to([128, 4, 64]), ALU.mult,
                        r=[yib.b(), tk[c].b(2)], w=[yi_.b()])
                    Snb = bank('A')
                    mm(Snb.ap[:, 0:256], xbg.ap[:, c, 256:384], xwg.ap[:, c, :], r=[xbg.b(c), xwg.b(c)], w=[Snb.b()])
                    vtt(S.ap[:, g, :].rearrange("p (h d) -> p h d", h=4),
                        S.ap[:, g, :].rearrange("p (h d) -> p h d", h=4),
                        tk[c].ap[:, 4, 4 * g:4 * g + 4].unsqueeze(2).broadcast_to([128, 4, 64]), ALU.mult,
                        r=[S.b(g), tk[c].b(4)], w=[S.b(g)])
                    vtt(S.ap[:, g, :], S.ap[:, g, :], Snb.ap[:, 0:256], ALU.add, r=[S.b(g), Snb.b()], w=[S.b(g)])
                    act(S_bf.ap[:, g, :], S.ap[:, g, :], AF.Copy, r=[S.b(g)], w=[S_bf.b(g)])
                    yield
                    yab = bank('A')
                    for j in range(4):
                        mm(yab.ap[:, j * 64:(j + 1) * 64], MT.ap[:, j * NCH + c, :], xbg.ap[:, c, j * 64:(j + 1) * 64],
                           start=True, stop=False, r=[MT.b(j * NCH + c), xbg.b(c)], w=[yab.b()])
                        mm(yab.ap[:, j * 64:(j + 1) * 64], DI.ap[:, j, :], xbg.ap[:, c, j * 64:(j + 1) * 64],
                           start=False, stop=True, r=[DI.b(j), xbg.b(c)], w=[yab.b()])
                    t2_ = t2[c % 2]
                    vtt(t2_.ap, yab.ap[:, 0:256], yi_.ap, ALU.add, r=[yab.b(), yi_.b()], w=[t2_.b()])
                    vtt(ygg.ap[:, c, :], t2_.ap, zc.ap[:, c, zoff:zoff + 256], ALU.mult,
                        r=[t2_.b(), zc.b(c)], w=[ygg.b(c)], E=POOL)
                    act(junk.ap, ygg.ap[:, c, :], AF.Square, accum=ssg.ap[:, par, c:c + 1],
                        r=[ygg.b(c)], w=[junk.b(), ssg.b(par)])
                    yield
                act(rsg.ap[:, par, :], ssg.ap[:, par, :], AF.Ln, bias=EPS, scale=1.0 / 256.0,
                    r=[ssg.b(par)], w=[rsg.b(par)])
                act(rsg.ap[:, par, :], rsg.ap[:, par, :], AF.Exp, scale=-0.5, r=[rsg.b(par)], w=[rsg.b(par)])
                for c in range(NCH):
                    vtt(yn.ap[:, c, :], ygg.ap[:, c, :], rsg.ap[:, par, c:c + 1].broadcast_to([128, 256]), ALU.mult,
                        r=[ygg.b(c), rsg.b(par)], w=[yn.b(c)], E=POOL)
                yield 'defer'
                tb = bank('A')
                tbb = tb.ap.bitcast(BF16)
                for jj in range(2):
                    for c in range(NCH):
                        trp(tbb[:, jj * 512 + c * 128: jj * 512 + (c + 1) * 128], yn.ap[:, c, jj * 128:(jj + 1) * 128],
                            ident_b.ap, r=[yn.b(c), ident_b.b()], w=[tb.b()])
                for jj in range(2):
                    col = PC_SNW + 2 * g + jj
                    vts(ynT.ap[:, 2 * g + jj, :], tbb[:, jj * 512:(jj + 1) * 512], pcols.ap[:, col:col + 1], None, ALU.mult,
                        r=[tb.b(), pcols.b()], w=[ynT.b(2 * g + jj)])

            def drain(gen):
                for _ in gen:
                    pass

            bk = bank('A')
            for k in range(8):
                mm(bk.ap[0:64, :], wdt.ap[:, k, :], uT.ap[:, k, :], start=(k == 0), stop=(k == 7),
                   r=[wdt.b(), uT.b(k)], w=[bk.b()])
            act(dc.ap[0:64, :], bk.ap[0:64, :], AF.Exp, bias=pcols.ap[0:64, PC_DTB:PC_DTB + 1],
                r=[bk.b(), pcols.b()], w=[dc.b()])
            act(dc.ap[0:64, :], dc.ap[0:64, :], AF.Ln, bias=1.0, r=[dc.b()], w=[dc.b()])
            chk(21)
            vts(atmp.ap[32:64, :], dc.ap[32:64, :], nA.ap[32:64, 0:1], None, ALU.mult,
                r=[dc.b(), nA.b()], w=[atmp.b()])
            chk(22)
            sc.op(DVE, lambda e: e.tensor_tensor_scan(dc.ap[32:64, :], cst.ap[32:64, C_RESET:C_RESET + T],
                                                      atmp.ap[32:64, :], 0.0, ALU.mult, ALU.add),
                  r=[cst.b(), atmp.b()], w=[dc.b()])
            chk(2)
            vcopy(csp.ap[32:64, 0, :], dc.ap[32:64, :], r=[dc.b()], w=[csp.b(0)])
            vtt(cres.ap[32:64, :], dc.ap[32:64, :], csp.ap[32:64, 0, :], ALU.subtract, r=[dc.b(), csp.b(0)], w=[cres.b()])
            vcopy(csp.ap[32:64, 1, :], cres.ap[32:64, :], r=[cres.b()], w=[csp.b(1)])
            chk(23)
            A0 = ssd_A(0)
            for _ in range(6):
                next(A0)
            for c in range(NCH):
                last = c * 128 + 127
                vts(diagc.ap[32:64, c, :], cst.ap[32:64, C_ID + 32:C_ID + 64], dc.ap[32:64, last:last + 1], None,
                    ALU.mult, r=[cst.b(), dc.b()], w=[diagc.b(c)])
            clb = bank('A')
            mm(clb.ap[:, 0:128], ones_f.ap[32:64, :], diagc.ap[32:64, :, :].rearrange("p a b -> p (a b)"),
               r=[ones_f.b(), diagc.b()], w=[clb.b()])
            chk(24)
            vcopy(clS.ap, clb.ap[:, 0:128], r=[clb.b()], w=[clS.b()])
            for c in range(NCH):
                tb = bank('A')
                trp(tb.ap[:, 0:128], dc.ap[:, c * 128:(c + 1) * 128], cst.ap[:, C_ID:C_ID + 128],
                    r=[dc.b(), cst.b()], w=[tb.b()])
                tkc = tk[c]
                vcopy(tkc.ap[:, 0:2, :], tb.ap[:, 0:64].rearrange("p (a b) -> p a b", a=2), r=[tb.b()], w=[tkc.b(0, 2)])
                if c == 0:
                    chk(25)
                act(tkc.ap[:, 2, :], tkc.ap[:, 1, :], AF.Exp, r=[tkc.b(1)], w=[tkc.b(2)])
                vtt(tkt.ap, clS.ap[:, c * 32:(c + 1) * 32], tkc.ap[:, 1, :], ALU.subtract,
                    r=[clS.b(), tkc.b(1)], w=[tkt.b()])
                vts(tkc.ap[:, 1, :], tkc.ap[:, 1, :], -1.0, None, ALU.mult, r=[tkc.b(1)], w=[tkc.b(1)])
                if c == 0:
                    chk(26)
                act(tkt.ap, tkt.ap, AF.Exp, r=[tkt.b()], w=[tkt.b()])
                if c == 0:
                    chk(27)
                vtt(tkc.ap[:, 3, :], tkt.ap, tkc.ap[:, 0, :], ALU.mult, r=[tkt.b(), tkc.b(0)], w=[tkc.b(3)])
                act(tkc.ap[:, 4, :], clS.ap[:, c * 32:(c + 1) * 32], AF.Exp, r=[clS.b()], w=[tkc.b(4)])
                if c == 0:
                    chk(28)
                if c == 1:
                    chk(29)
            if ti == 0:
                tap("dc", dc.ap[0:64, :], dc.b())
                tap("tk0", tk[0].ap.rearrange("p a b -> p (a b)"), tk[0].b())

            chk(3)

            drain(A0)
            for g in range(8):
                gb = ssd_B(g)
                ga = ssd_A(g + 1) if g < 7 else iter(())
                done_a = done_b = False
                hold = 0
                while not (done_a and done_b):
                    if not done_a:
                        try:
                            next(ga)
                        except StopIteration:
                            done_a = True
                    if not done_b:
                        if hold > 0 and not done_a:
                            hold -= 1
                        else:
                            try:
                                if next(gb) == 'defer':
                                    hold = 3
                            except StopIteration:
                                done_b = True
            if ti == 0:
                tap("ynT", ynT.ap.rearrange("p a b -> p (a b)"), ynT.b())
            chk(9)
            oa = OutAcc()
            for b_ in range(4):
                sl = w_next()
                w3 = sl.ap.rearrange("p (k j) -> p k j", k=16)
                for o2 in range(2):
                    oc = 2 * b_ + o2
                    bk = bank('A')
                    for k in range(16):
                        mm(bk.ap, w3[:, k, o2 * 128:(o2 + 1) * 128], ynT.ap[:, k, :], start=(k == 0), stop=(k == 15),
                           r=[sl.b(), ynT.b(k)], w=[bk.b()])
                    oa.chunk(oc, bk)
            oa.finish()
            if ti == 0:
                tap("mT0", mT.ap.rearrange("p a b -> p (a b)"), mT.b())
            post_norm(1)
            if ti == 0:
                tap("h1", hT.ap.rearrange("p a b -> p (a b)"), hT.b())

        def attn_layer(ti):
            pre_norm(4)
            for b_ in range(2):
                sl = w_next()
                w3 = sl.ap.rearrange("p (k j) -> p k j", k=8)
                for j in range(4):
                    qc = 4 * b_ + j
                    bk = bank('A')
                    for k in range(8):
                        mm(bk.ap, w3[:, k, j * 128:(j + 1) * 128], uT.ap[:, k, :], start=(k == 0), stop=(k == 7),
                           r=[sl.b(), uT.b(k)], w=[bk.b()])
                    col = PC_QB + qc
                    act(qT.ap[:, qc, :], bk.ap, AF.Identity, bias=pcols.ap[:, col:col + 1],
                        r=[bk.b(), pcols.b()], w=[qT.b(qc)])
            sl = w_next()
            w3 = sl.ap.rearrange("p (k j) -> p k j", k=8)
            vcopy(kT.ap[:, :, 0:128], kprev.ap, r=[kprev.b()], w=[kT.b()])
            for j in range(4):
                bk = bank('A')
                for k in range(8):
                    mm(bk.ap, w3[:, k, j * 128:(j + 1) * 128], uT.ap[:, k, :], start=(k == 0), stop=(k == 7),
                       r=[sl.b(), uT.b(k)], w=[bk.b()])
                col = PC_KB + j
                act(kT.ap[:, j, 128:128 + T], bk.ap, AF.Identity, bias=pcols.ap[:, col:col + 1],
                    r=[bk.b(), pcols.b()], w=[kT.b(j)])
            vcopy(kprev.ap, kT.ap[:, :, T:T + 128], r=[kT.b()], w=[kprev.b()])
            sl = w_next()
            w3 = sl.ap.rearrange("p (k j) -> p k j", k=8)
            vcopy(Vt.ap[:, 0, :], vprev.ap, r=[vprev.b()], w=[Vt.b(0)])
            for c in range(NCH):
                bk = bank('A')
                for k in range(8):
                    mm(bk.ap[:, 0:256], uT.ap[:, k, c * 128:(c + 1) * 128], w3[:, k, 0:256], start=(k == 0), stop=(k == 7),
                       r=[uT.b(k), sl.b()], w=[bk.b()])
                vtt(Vt.ap[:, 1 + c, :], bk.ap[:, 0:256], prep.ap[:, PR_BV:PR_BV + 256], ALU.add,
                    r=[bk.b(), prep.b()], w=[Vt.b(1 + c)])
            vcopy(vprev.ap, Vt.ap[:, NCH, :], r=[Vt.b(NCH)], w=[vprev.b()])
            obs = {}
            Sbs = {}

            def st1(c, kv):
                gb = ti * NCH + c
                par = c % 2
                mo = 256 if gb == 0 else 0
                Sb = [bank('A'), bank('A')]
                Sbs[(c, kv)] = Sb
                for jp in range(2):
                    for j in (2 * jp, 2 * jp + 1):
                        h = 4 * kv + j
                        qc, pb = h // 2, 64 * (h % 2)
                        bj = Sb[j % 2]
                        c0 = (j // 2) * 256
                        mm(bj.ap[:, c0:c0 + 256], qT.ap[pb:pb + 64, qc, c * 128:(c + 1) * 128],
                           kT.ap[pb:pb + 64, kv, c * 128:c * 128 + 256],
                           start=True, stop=False, r=[qT.b(qc), kT.b(kv)], w=[bj.b()])
                    for j in (2 * jp, 2 * jp + 1):
                        bj = Sb[j % 2]
                        c0 = (j // 2) * 256
                        mm(bj.ap[:, c0:c0 + 256], ident_b.ap, matt_b.ap[:, mo:mo + 256], start=False, stop=True,
                           r=[ident_b.b(), matt_b.b()], w=[bj.b()])

            def st1b(c, kv):
                par = c % 2
                Sb = Sbs[(c, kv)]
                for i in range(2):
                    sc.op(DVE, lambda e, o=mr.ap[:, par, 4 * kv + i:4 * kv + i + 3:2],
                          i_=Sb[i].ap.rearrange("p (a b) -> p a b", a=2): e.reduce_max(o, i_, AX.X),
                          r=[Sb[i].b()], w=[mr.b(par)])
                vstt(negm.ap[:, par, 4 * kv:4 * kv + 4], mr.ap[:, par, 4 * kv:4 * kv + 4], -0.125,
                     nsink.ap[:, 4 * kv:4 * kv + 4], ALU.mult, ALU.min,
                     r=[mr.b(par), nsink.b()], w=[negm.b(par)])
                P_ = P4[kv % 2]
                for j in range(4):
                    h = 4 * kv + j
                    bj = Sb[j % 2]
                    c0 = (j // 2) * 256
                    act(P_.ap[:, j, :], bj.ap[:, c0:c0 + 256], AF.Exp, bias=negm.ap[:, par, h:h + 1], scale=0.125,
                        accum=rsum.ap[:, par, h:h + 1], r=[bj.b(), negm.b(par)], w=[P_.b(j), rsum.b(par)])

            def st2(c, kv):
                if kv == 0:
                    obs[c] = [bank('B'), bank('B')]
                ob = obs[c]
                P_ = P4[kv % 2]
                tb = bank('C')
                tbb = tb.ap.bitcast(BF16)
                for j in range(4):
                    for i in range(2):
                        trp(tbb[:, (2 * j + i) * 128:(2 * j + i + 1) * 128], P_.ap[:, j, i * 128:(i + 1) * 128], ident_b.ap,
                            r=[P_.b(j), ident_b.b()], w=[tb.b()])
                PT_ = PT4[kv % 2]
                vcopy(PT_.ap, tbb, r=[tb.b()], w=[PT_.b()])
                for j in range(4):
                    h = 4 * kv + j
                    o_ = ob[h // 8]
                    oc0 = (h % 8) * 64
                    mm(o_.ap[:, oc0:oc0 + 64], PT_.ap[:, (2 * j) * 128:(2 * j + 1) * 128], Vt.ap[:, c, kv * 64:(kv + 1) * 64],
                       start=True, stop=False, r=[PT_.b(), Vt.b(c)], w=[o_.b()])
                    mm(o_.ap[:, oc0:oc0 + 64], PT_.ap[:, (2 * j + 1) * 128:(2 * j + 2) * 128],
                       Vt.ap[:, c + 1, kv * 64:(kv + 1) * 64],
                       start=False, stop=True, r=[PT_.b(), Vt.b(c + 1)], w=[o_.b()])

            def st3(c):
                par = c % 2
                ob = obs[c]
                vtt(esk.ap[:, par, :], negm.ap[:, par, :], nsink.ap, ALU.subtract, r=[negm.b(par), nsink.b()], w=[esk.b(par)])
                act(esk.ap[:, par, :], esk.ap[:, par, :], AF.Exp, r=[esk.b(par)], w=[esk.b(par)])
                vtt(rden.ap[:, par, :], esk.ap[:, par, :], rsum.ap[:, par, :], ALU.add, r=[esk.b(par), rsum.b(par)], w=[rden.b(par)])
                sc.op(DVE, lambda e, o=rden.ap[:, par, :]: e.reciprocal(o, o), r=[rden.b(par)], w=[rden.b(par)])
                ao_ = ao[par]
                for i in range(2):
                    vtt(ao_.ap[:, i * 512:(i + 1) * 512].rearrange("p (h d) -> p h d", h=8),
                        ob[i].ap.rearrange("p (h d) -> p h d", h=8),
                        rden.ap[:, par, 8 * i:8 * i + 8].unsqueeze(2).broadcast_to([128, 8, 64]), ALU.mult,
                        r=[ob[i].b(), rden.b(par)], w=[ao_.b()])
                tb = bank('C')
                tbb = tb.ap.bitcast(BF16)
                for f in range(8):
                    trp(tbb[:, f * 128:(f + 1) * 128], ao_.ap[:, f * 128:(f + 1) * 128], ident_b.ap,
                        r=[ao_.b(), ident_b.b()], w=[tb.b()])
                act(aoT.ap[:, :, c * 128:(c + 1) * 128], tbb.rearrange("p (f t) -> p f t", f=8), AF.Copy,
                    r=[tb.b()], w=[aoT.b()])

            grps = [(c, kv) for c in range(NCH) for kv in range(4)]
            st1(*grps[0])
            st1b(*grps[0])
            for gi, (c, kv) in enumerate(grps):
                if gi + 1 < len(grps):
                    st1(*grps[gi + 1])
                st2(c, kv)
                if gi + 1 < len(grps):
                    st1b(*grps[gi + 1])
                if kv == 3:
                    st3(c)
            if ti == 0:
                tap("aoT", aoT.ap.rearrange("p a b -> p (a b)"), aoT.b())
            oa = OutAcc(bias_col=PC_OB)
            for b_ in range(2):
                sl = w_next()
                w3 = sl.ap.rearrange("p (k j) -> p k j", k=8)
                for j in range(4):
                    oc = 4 * b_ + j
                    bk = bank('A')
                    for k in range(8):
                        mm(bk.ap, w3[:, k, j * 128:(j + 1) * 128], aoT.ap[:, k, :], start=(k == 0), stop=(k == 7),
                           r=[sl.b(), aoT.b(k)], w=[bk.b()])
                    oa.chunk(oc, bk)
            oa.finish()
            post_norm(5)

        prefetched = set()

        def x_dma(ti, c):
            if (ti, c) in prefetched or ti >= nt:
                return
            prefetched.add((ti, c))
            xi = xin[c % 2]
            r0 = ti * T + c * 128
            sc.dma(SP, lambda e, xi=xi, r0=r0: e.dma_start(out=xi.ap, in_=x_d[r0:r0 + 128, :]), ds_xin[c % 2], w=[xi.b()])

        def load_tile(ti):
            for c in range(NCH):
                xi = xin[c % 2]
                x_dma(ti, c)
                for half in range(2):
                    tb = bank('A')
                    for i in range(4):
                        f = 4 * half + i
                        trp(tb.ap[:, i * 128:(i + 1) * 128], xi.ap[:, f * 128:(f + 1) * 128], cst.ap[:, C_ID:C_ID + 128],
                            r=[xi.b(), cst.b()], w=[tb.b()])
                    act(hT.ap[:, 4 * half:4 * half + 4, c * 128:(c + 1) * 128], tb.ap.rearrange("p (f t) -> p f t", f=4),
                        AF.Copy, r=[tb.b()], w=[hT.b(4 * half, 4 * half + 4)])

        def store_tile(ti, src):
            for c in range(NCH):
                xo = xout[c % 2]
                for half in range(2):
                    tb = bank('A')
                    for i in range(4):
                        f = 4 * half + i
                        trp(tb.ap[:, i * 128:(i + 1) * 128], src.ap[:, f, c * 128:(c + 1) * 128], cst.ap[:, C_ID:C_ID + 128],
                            r=[src.b(f), cst.b()], w=[tb.b()])
                    act(xo.ap[:, half * 512:(half + 1) * 512], tb.ap, AF.Copy, r=[tb.b()], w=[xo.b()])
                r0 = ti * T + c * 128
                tokn = sc.dma(SP, lambda e, xo=xo, r0=r0: e.dma_start(out=o_d[r0:r0 + 128, :], in_=xo.ap), ds_xout[c % 2], r=[xo.b()])
                out_tokens.append(tokn)

        last_layer = max(layers) if layers else None
        for ti in range(nt):
            if ti == 0:
                load_tile(ti)
            src = hT
            try:
                if 0 in layers:
                    ssd_layer(ti)
                    chk(10)
                    mlp(2, 3, dst=(mT if last_layer == 0 else None))
                    if last_layer == 0:
                        src = mT
                if 1 in layers:
                    attn_layer(ti)
                    x_dma(ti + 1, 0)
                    x_dma(ti + 1, 1)
                    mlp(6, 7, dst=mT)
                    src = mT
            except _Stop:
                pass
            if ti + 1 < nt and src is mT:
                load_tile(ti + 1)
                store_tile(ti, src)
            else:
                store_tile(ti, src)
                if ti + 1 < nt:
                    load_tile(ti + 1)
        sc.wait_tokens(SP, out_tokens)

        with nc.Block() as block:
            sc.replay(nc, block)
    return nc


def _wblock(W, cols, nk):
    sub = W[:, cols]
    a = sub.reshape(nk, 128, len(cols)).transpose(1, 0, 2)
    return np.ascontiguousarray(a).reshape(128, nk * len(cols))


def _xbc_cols(g):
    x0 = 2048 + g * 256
    b0 = 2048 + 2048 + g * 128
    c0 = 2048 + 3072 + g * 128
    return np.concatenate([np.arange(x0, x0 + 256), np.arange(b0, b0 + 128), np.arange(c0, c0 + 128)])


def host_prep(inp):
    f = np.float32
    w_in = np.asarray(inp["ssd_w_in"][0], f)
    ws = np.zeros((NBLK0 + NBLK1, 128, WBLK), f)
    bi = 0
    for b in range(4):
        ws[bi] = _wblock(w_in, np.arange(b * 512, (b + 1) * 512), 8); bi += 1
        for g in (2 * b, 2 * b + 1):
            ws[bi] = _wblock(w_in, _xbc_cols(g), 8); bi += 1
    w_out = np.asarray(inp["ssd_w_out"][0], f)
    for b in range(4):
        ws[bi] = _wblock(w_out, np.arange(b * 256, (b + 1) * 256), 16); bi += 1
    for layer in range(2):
        if layer == 1:
            wq = np.asarray(inp["attn_w_qkv"][0], f)
            for b in range(2):
                ws[bi] = _wblock(wq, np.arange(b * 512, (b + 1) * 512), 8); bi += 1
            kc = np.concatenate([1024 + jj * 64 + (np.arange(128) % 64) for jj in range(4)])
            ws[bi] = _wblock(wq, kc, 8); bi += 1
            ws[bi][:, :] = 0
            ws[bi].reshape(128, 8, 512)[:, :, 0:256] = _wblock(wq, np.arange(1280, 1536), 8).reshape(128, 8, 256); bi += 1
            wo = np.asarray(inp["attn_w_o"][0], f)
            for b in range(2):
                ws[bi] = _wblock(wo, np.arange(b * 512, (b + 1) * 512), 8); bi += 1
        wu = np.asarray(inp["mlp_w_up"][layer], f)
        wd = np.asarray(inp["mlp_w_down"][layer], f)
        for b in range(8):
            ws[bi] = _wblock(wu, np.arange(b * 512, (b + 1) * 512), 8); bi += 1
        for oc in range(8):
            ws[bi] = _wblock(wd, np.arange(oc * 128, (oc + 1) * 128), 32); bi += 1
    assert bi == NBLK0 + NBLK1
    dtc = 6144 + (np.arange(64) % 32)
    wdt = _wblock(w_in, dtc, 8)
    pc = np.zeros((128, NPC), f)
    norms = [inp["mix_pre_norm"][0], inp["mix_post_norm"][0], inp["ffn_pre_norm"][0], inp["ffn_post_norm"][0],
             inp["mix_pre_norm"][1], inp["mix_post_norm"][1], inp["ffn_pre_norm"][1], inp["ffn_post_norm"][1]]
    for n, wv in enumerate(norms):
        pc[:, PC_NW + n * 8:PC_NW + n * 8 + 8] = np.asarray(wv, f).reshape(8, 128).T
    cb = np.asarray(inp["ssd_conv_b"][0], f)
    cw = np.asarray(inp["ssd_conv_w"][0], f)
    for g in range(8):
        cols = _xbc_cols(g) - 2048
        for j in range(4):
            ch = cols[j * 128:(j + 1) * 128]
            pc[:, PC_CONVB + 4 * g + j] = cb[ch]
            for k in range(4):
                pc[:, PC_CONVW + (4 * g + j) * 4 + k] = cw[k, ch]
    bq = np.asarray(inp["attn_b_qkv"][0], f)
    pc[:, PC_QB:PC_QB + 8] = bq[0:1024].reshape(8, 128).T
    for jj in range(4):
        pc[:, PC_KB + jj] = bq[1024 + jj * 64 + (np.arange(128) % 64)]
    pc[:, PC_OB:PC_OB + 8] = np.asarray(inp["attn_b_o"][0], f).reshape(8, 128).T
    pc[:, PC_DTB] = np.asarray(inp["ssd_dt_bias"][0], f)[np.arange(128) % 32]
    pc[:, PC_ALOG] = np.asarray(inp["ssd_a_log"][0], f)[np.arange(128) % 32]
    pc[:, PC_SNW:PC_SNW + 16] = np.asarray(inp["ssd_norm_w"][0], f).reshape(16, 128).T
    pr = np.zeros((128, NPR), f)
    pr[:, PR_D:PR_D + 32] = np.asarray(inp["ssd_d"][0], f)[None, :]
    pr[:, PR_BV:PR_BV + 256] = bq[1280:1536][None, :]
    pr[:, PR_SINK:PR_SINK + 16] = np.asarray(inp["attn_sinks"][0], f)[None, :]
    cs = np.zeros((128, NCST), f)
    cs[:, C_ID:C_ID + 128] = np.eye(128, dtype=f)
    s_ = np.arange(128)[:, None]
    t_ = np.arange(128)[None, :]
    m = np.where(t_ >= s_, 0.0, NEG).astype(f)
    mk = np.zeros((128, 1024), f)
    mk[:, 0:512] = np.tile(m, (1, 4))
    q_ = np.arange(128)[:, None]
    k_ = np.arange(128)[None, :]
    prev = np.where(k_ > q_, 0.0, NEG).astype(f)
    cur = np.where(k_ <= q_, 0.0, NEG).astype(f)
    mk[:, 512:640] = prev
    mk[:, 640:768] = cur
    mk[:, 768:896] = NEG
    mk[:, 896:1024] = cur
    rm = np.ones((T,), f)
    rm[0::128] = 0.0
    cs[:, C_RESET:C_RESET + T] = rm[None, :]
    return {"wstream": ws, "wdt": wdt, "pcols": pc, "prep": pr, "cst": cs, "msk": mk}


_CACHE = {}


def _get_nc(layers):
    key = tuple(layers)
    if key not in _CACHE:
        _CACHE[key] = build(layers=key)
    return _CACHE[key]


FUSED = True


def kernel(**inputs):
    x = np.asarray(inputs["x"], np.float32)
    hp = host_prep(inputs)
    B = x.shape[0]
    cur = [np.ascontiguousarray(x[b]) for b in range(B)]
    plans = [(0, 1)] if FUSED else [(0,), (1,)]
    for layers in plans:
        nc = _get_nc(layers)
        in_maps = [dict(hp, x=cur[b]) for b in range(B)]
        res = run_bass_kernel_spmd(nc, in_maps, core_ids=list(range(B)))
        cur = [np.asarray(res.results[b]["out"], np.float32) for b in range(B)]
    return np.stack(cur, axis=0).astype(np.float32)
```

```python
import numpy as np
from contextlib import ExitStack
import concourse.bass as bass
import concourse.mybir as mybir
from concourse.bass_utils import run_bass_kernel_spmd

F32, BF16 = mybir.dt.float32, mybir.dt.bfloat16
AF = mybir.ActivationFunctionType
ALU = mybir.AluOpType
AX = mybir.AxisListType

T = 512
NCH = 4
L = 4096
NT = L // T
DM = 1024
EPS = 1e-6
NEG = -30000.0
WBLK = 4096
NSLOT = 4
NBLK0 = 32
NBLK1 = 22

PC_NW = 0
PC_CONVB = 64
PC_CONVW = 96
PC_QB = 224
PC_KB = 232
PC_OB = 236
PC_DTB = 244
PC_ALOG = 245
PC_SNW = 246
NPC = 262
PR_D = 0
PR_BV = 32
PR_SINK = 288
NPR = 304
C_ID = 0
C_RESET = 128
NCST = 640


class Region:
    def __init__(self, name):
        self.name = name
        self.recs = []


class Buf:
    __slots__ = ("reg", "lo", "hi")

    def __init__(self, reg, lo, hi):
        self.reg, self.lo, self.hi = reg, lo, hi


class Eng:
    def __init__(self, name, ordered):
        self.name = name
        self.q = []
        self.seen = {}
        self.ordered = ordered
        self.sem = None


class Ins:
    __slots__ = ("fn", "waits", "needed", "dsem")

    def __init__(self, fn, waits, dsem=None):
        self.fn, self.waits, self.needed, self.dsem = fn, waits, False, dsem


class X:
    def __init__(self, ap, reg, base, esz, fshape):
        self.ap, self.reg, self.base, self.esz, self.fshape = ap, reg, base, esz, list(fshape)
        st = 1
        for d in fshape[1:]:
            st *= d
        self.st0 = st * esz
        self.nbytes = st * fshape[0] * esz

    def b(self, i0=None, i1=None):
        if i0 is None:
            return Buf(self.reg, self.base, self.base + self.nbytes)
        if i1 is None:
            i1 = i0 + 1
        return Buf(self.reg, self.base + i0 * self.st0, self.base + i1 * self.st0)


class Sched:
    def __init__(self):
        self.PE = Eng("pe", True)
        self.ACT = Eng("act", False)
        self.DVE = Eng("dve", False)
        self.POOL = Eng("pool", False)
        self.SP = Eng("sp", True)
        self.engs = [self.PE, self.ACT, self.DVE, self.POOL, self.SP]

    def _deps(self, E, reads, writes, tok):
        deps = []
        for is_w, lst in ((False, reads), (True, writes)):
            for bf in lst:
                lo, hi = bf.lo, bf.hi
                for r in bf.reg.recs:
                    rlo, rhi, rw, rt = r
                    if rhi <= lo or rlo >= hi:
                        continue
                    if is_w or rw:
                        deps.append(rt)
        for is_w, lst in ((False, reads), (True, writes)):
            for bf in lst:
                reg, lo, hi = bf.reg, bf.lo, bf.hi
                new = []
                for r in reg.recs:
                    rlo, rhi, rw, rt = r
                    if rhi <= lo or rlo >= hi:
                        new.append(r)
                        continue
                    cov = lo <= rlo and rhi <= hi
                    if rt is tok:
                        if not cov:
                            new.append(r)
                        continue
                    if is_w and cov:
                        continue
                    if (not is_w) and (not rw) and cov and rt[0] == 'e' and tok[0] == 'e' and rt[1] is tok[1]:
                        continue
                    new.append(r)
                new.append((lo, hi, is_w, tok))
                reg.recs = new
        return deps

    def _waits(self, E, deps):
        waits = {}
        for d in deps:
            if d[0] == 'e':
                _, E2, idx = d
                if E2 is E and E.ordered:
                    continue
                key = E2.name
                if E.seen.get(key, -1) >= idx:
                    continue
                if key not in waits or waits[key][2] < idx:
                    waits[key] = d
            else:
                _, sem, val, key = d
                if E.seen.get(key, -1) >= val:
                    continue
                if key not in waits or waits[key][2] < val:
                    waits[key] = d
        out = []
        for key, d in waits.items():
            E.seen[key] = d[2]
            if d[0] == 'e':
                d[1].q[d[2]].needed = True
            out.append(d)
        return out

    def op(self, E, fn, r=(), w=()):
        idx = len(E.q)
        tok = ('e', E, idx)
        deps = self._deps(E, r, w, tok)
        E.q.append(Ins(fn, self._waits(E, deps)))
        return tok

    def dma(self, E, fn, dsem, r=(), w=()):
        dsem['val'] += 16
        tok = ('d', dsem['sem'], dsem['val'], dsem['key'])
        deps = self._deps(E, r, w, tok)
        E.q.append(Ins(fn, self._waits(E, deps), dsem=dsem['sem']))
        return tok

    def wait_tokens(self, E, toks):
        E.q.append(Ins(None, self._waits(E, toks)))

    def replay(self, nc, block):
        pref = {}
        for E in self.engs:
            c = 0
            p = []
            for ins in E.q:
                if ins.needed:
                    c += 1
                p.append(c)
            pref[E.name] = p

        def run(E, e):
            fuse = E.name in ("act", "dve")
            for ins in E.q:
                ws = []
                for d in ins.waits:
                    if d[0] == 'e':
                        ws.append((d[1].sem, pref[d[1].name][d[2]]))
                    else:
                        ws.append((d[1], d[2]))
                last = None
                if fuse and ins.fn is not None and ws:
                    last = ws.pop()
                for sm, v in ws:
                    e.wait_ge(sm, v)
                if ins.fn is None:
                    continue
                bi = ins.fn(e)
                if last is not None:
                    bi._wait_ge(last[0], last[1])
                if ins.dsem is not None:
                    bi.then_inc(ins.dsem, 16)
                elif ins.needed:
                    bi.then_inc(E.sem, 1)

        S = self

        @block.tensor
        def _(e):
            run(S.PE, e)

        @block.scalar
        def _(e):
            run(S.ACT, e)

        @block.vector
        def _(e):
            run(S.DVE, e)

        @block.gpsimd
        def _(e):
            run(S.POOL, e)

        @block.sync
        def _(e):
            run(S.SP, e)


class _Stop(Exception):
    pass


def build(layers=(0, 1), nt=NT, taps=None, stop=None):
    taps = taps or {}

    def chk(n):
        if stop == n:
            raise _Stop()
    nc = bass.Bass("TRN2", target_bir_lowering=False)
    x_d = nc.dram_tensor("x", [L, DM], F32, kind="ExternalInput").ap()
    o_d = nc.dram_tensor("out", [L, DM], F32, kind="ExternalOutput").ap()
    ws_d = nc.dram_tensor("wstream", [NBLK0 + NBLK1, 128, WBLK], F32, kind="ExternalInput").ap()
    wdt_d = nc.dram_tensor("wdt", [128, 512], F32, kind="ExternalInput").ap()
    pc_d = nc.dram_tensor("pcols", [128, NPC], F32, kind="ExternalInput").ap()
    pr_d = nc.dram_tensor("prep", [128, NPR], F32, kind="ExternalInput").ap()
    cst_d = nc.dram_tensor("cst", [128, NCST], F32, kind="ExternalInput").ap()
    msk_d = nc.dram_tensor("msk", [128, 1024], F32, kind="ExternalInput").ap()
    tap_d = {}
    for name, shp in taps.items():
        tap_d[name] = nc.dram_tensor("tap_" + name, list(shp), F32, kind="ExternalOutput").ap()

    sc = Sched()
    PE, ACT, DVE, POOL, SP = sc.PE, sc.ACT, sc.DVE, sc.POOL, sc.SP

    with ExitStack() as es:
        def sb(name, fshape, dt):
            t = es.enter_context(nc.sbuf_tensor("sb_" + name, [128] + list(fshape), dt))
            esz = 4 if dt == F32 else 2
            return X(t[:], Region(name), 0, esz, fshape)

        def sem(name):
            return es.enter_context(nc.semaphore(name))

        for E in sc.engs:
            E.sem = sem("s_" + E.name)

        def dsem(name):
            return {'sem': sem(name), 'val': 0, 'key': name}

        hT = sb("hT", [8, T], F32)
        uT = sb("uT", [8, T], BF16)
        mT = sb("mT", [8, T], F32)
        xin = [sb("xin%d" % i, [DM], F32) for i in range(2)]
        xout = [sb("xout%d" % i, [DM], F32) for i in range(2)]
        wsl = [sb("wsl%d" % i, [WBLK], BF16) for i in range(NSLOT)]
        sq = [sb("sq%d" % i, [T], BF16) for i in range(2)]
        rstd = sb("rstd", [T], F32)
        lnt = rstd
        cst = sb("cst", [NCST], F32)
        ident_b = sb("ident_b", [128], BF16)
        ones_b = sb("ones_b", [128], BF16)
        ones_f = sb("ones_f", [128], F32)
        mscan_b = sb("mscan_b", [512], BF16)
        matt_b = sb("matt_b", [512], BF16)
        pcols = sb("pcols", [NPC], F32)
        prep = sb("prep", [NPR], F32)
        nsink = sb("nsink", [16], F32)
        nA = sb("nA", [1], F32)
        wdt = sb("wdt", [8, 64], BF16)
        S = sb("S", [8, 256], F32)
        S_bf = sb("S_bf", [8, 256], BF16)
        tails = sb("tails", [32, 3], BF16)
        dc = sb("dc", [T], F32)
        atmp = sb("atmp", [T], F32)
        csp = sb("csp", [3, T], BF16)
        cres = sb("cres", [T], F32)
        tk = [sb("tk%d" % c, [5, 32], F32) for c in range(NCH)]
        tkt = sb("tkt", [32], F32)
        diagc = sb("diagc", [NCH, 32], F32)
        clS = sb("clS", [128], F32)
        kprev = sb("kprev", [4, 128], BF16)
        vprev = sb("vprev", [256], BF16)
        ssg = sb("ssg", [2, 4], F32)
        rsg = sb("rsg", [2, 4], F32)
        junk = sb("junk", [256], BF16)
        mhalf = sb("mhalf", [4], F32)
        mr = sb("mr", [2, 16], F32)
        negm = sb("negm", [2, 16], F32)
        rsum = sb("rsum", [2, 16], F32)
        esk = sb("esk", [2, 16], F32)
        rden = sb("rden", [2, 16], F32)

        ARENA = 78 * 1024
        arena_t = es.enter_context(nc.sbuf_tensor("arena", [128, ARENA // 2], BF16))
        arena_reg = Region("arena")
        aoff = [0]

        def carve(fshape, dt, at=None):
            esz = 4 if dt == F32 else 2
            n = 1
            for d in fshape:
                n *= d
            nb = n * esz
            if at is None:
                at = aoff[0]
                aoff[0] += nb
            assert at % 4 == 0 and at + nb <= ARENA, (at, nb)
            ap = arena_t[:, at // 2:(at + nb) // 2]
            if dt == F32:
                ap = ap.bitcast(F32)
            if len(fshape) == 2:
                ap = ap.rearrange("p (a b) -> p a b", a=fshape[0])
            elif len(fshape) == 3:
                ap = ap.rearrange("p (a b c) -> p a b c", a=fshape[0], b=fshape[1])
            return X(ap, arena_reg, at, esz, fshape)

        aoff[0] = 0
        zs = [carve([NCH, 512], BF16) for _ in range(2)]
        xraw = [carve([4, T + 8], BF16) for _ in range(2)]
        xc = [carve([4, T], BF16) for _ in range(2)]
        xbt = [carve([NCH, 384], BF16) for _ in range(2)]
        xw = [carve([NCH, 256], BF16) for _ in range(2)]
        dg = carve([16, 128], BF16)
        DIg = [carve([4, 128], BF16) for _ in range(2)]
        Ebuf = [carve([128], F32) for _ in range(4)]
        MTb = [carve([16, 128], BF16) for _ in range(2)]
        yi = [carve([256], F32) for _ in range(2)]
        t2 = [carve([256], F32) for _ in range(2)]
        yg = [carve([NCH, 256], BF16) for _ in range(2)]
        yn = carve([NCH, 256], BF16)
        ynT = carve([16, T], BF16)
        ssd_end = aoff[0]
        aoff[0] = 0
        aT = carve([32, T], BF16)
        rl = [carve([T], BF16) for _ in range(2)]
        mlp_end = aoff[0]
        aoff[0] = 0
        qT = carve([8, T], BF16)
        kT = carve([4, 128 + T], BF16)
        Vt = carve([NCH + 1, 256], BF16)
        P4 = [carve([4, 256], BF16) for _ in range(2)]
        PT4 = [carve([1024], BF16) for _ in range(2)]
        ao = [carve([1024], BF16) for _ in range(2)]
        aoT = carve([8, T], BF16)
        att_end = aoff[0]
        assert max(ssd_end, mlp_end, att_end) <= ARENA

        ps = []
        for i in range(8):
            t = es.enter_context(nc.psum_tensor("ps%d" % i, [128, 512], F32))
            ps.append(X(t[:], Region("ps%d" % i), 0, 4, [512]))
        rot = {'A': [0, 1, 2, 3], 'B': [4, 5], 'C': [6, 7]}
        roti = {'A': 0, 'B': 0, 'C': 0}

        def bank(pool='A'):
            i = rot[pool][roti[pool] % len(rot[pool])]
            roti[pool] += 1
            return ps[i]

        ds_w = [dsem("dw%d" % i) for i in range(NSLOT)]
        ds_xin = [dsem("dxi%d" % i) for i in range(2)]
        ds_xout = [dsem("dxo%d" % i) for i in range(2)]
        ds_c = dsem("dcst")
        ds_tap = dsem("dtap")

        def mm(out, lhsT, rhs, start=True, stop=True, r=(), w=()):
            return sc.op(PE, lambda e: e.matmul(out, lhsT, rhs, start=start, stop=stop), r, w)

        def trp(out, in_, ident, r=(), w=()):
            return sc.op(PE, lambda e: e.transpose(out, in_, ident), r, w)

        def act(out, in_, func, bias=None, scale=None, accum=None, r=(), w=()):
            kw = {}
            if bias is not None:
                kw['bias'] = bias
            if scale is not None:
                kw['scale'] = scale
            if accum is not None:
                kw['accum_out'] = accum
            return sc.op(ACT, lambda e: e.activation(out, in_, func, **kw), r, w)

        def vtt(out, in0, in1, op_, r=(), w=(), E=None):
            return sc.op(E or DVE, lambda e: e.tensor_tensor(out, in0, in1, op_), r, w)

        def vts(out, in0, s1, s2, op0, op1=None, r=(), w=(), E=None):
            if op1 is None:
                return sc.op(E or DVE, lambda e: e.tensor_scalar(out, in0, s1, None, op0), r, w)
            return sc.op(E or DVE, lambda e: e.tensor_scalar(out, in0, s1, s2, op0, op1), r, w)

        def vstt(out, in0, scalar, in1, op0, op1, r=(), w=()):
            return sc.op(DVE, lambda e: e.scalar_tensor_tensor(out, in0, scalar, in1, op0, op1), r, w)

        def vcopy(out, in_, r=(), w=(), E=None):
            return sc.op(E or DVE, lambda e: e.tensor_copy(out, in_), r, w)

        def vmemset(ap, val, w=(), E=None):
            return sc.op(E or DVE, lambda e: e.memset(ap, val), (), w)

        out_tokens = []

        def tap(name, src_ap, rb):
            if name in tap_d:
                tokn = sc.dma(POOL, lambda e: e.dma_start(out=tap_d[name], in_=src_ap), dsem("dtap_" + name), r=[rb])
                out_tokens.append(tokn)

        def ld(dst, src_ap):
            sc.dma(SP, lambda e: e.dma_start(out=dst.ap, in_=src_ap), dsem("dld_" + dst.reg.name), w=[dst.b()])

        ld(cst, cst_d[:, :])
        ld(pcols, pc_d[:, :])
        ld(prep, pr_d[:, :])
        sc.dma(POOL, lambda e: e.dma_start(out=wdt.ap.rearrange("p a b -> p (a b)"), in_=wdt_d[:, :]),
               dsem("dld_wdt"), w=[wdt.b()])
        vcopy(ident_b.ap, cst.ap[:, C_ID:C_ID + 128], r=[cst.b()], w=[ident_b.b()])
        sc.dma(POOL, lambda e: e.dma_start(out=mscan_b.ap, in_=msk_d[:, 0:512]), dsem("dld_mscan"), w=[mscan_b.b()])
        sc.dma(POOL, lambda e: e.dma_start(out=matt_b.ap, in_=msk_d[:, 512:1024]), dsem("dld_matt"), w=[matt_b.b()])
        vmemset(ones_b.ap, 1.0, w=[ones_b.b()])
        vmemset(ones_f.ap, 1.0, w=[ones_f.b()])
        vmemset(mhalf.ap, -0.5, w=[mhalf.b()])
        vmemset(S.ap, 0.0, w=[S.b()])
        vmemset(S_bf.ap, 0.0, w=[S_bf.b()])
        vmemset(tails.ap, 0.0, w=[tails.b()])
        vmemset(kprev.ap, 0.0, w=[kprev.b()])
        vmemset(vprev.ap, 0.0, w=[vprev.b()])
        vmemset(atmp.ap, 0.0, w=[atmp.b()])
        vmemset(dc.ap, 0.0, w=[dc.b()])
        vmemset(diagc.ap, 0.0, w=[diagc.b()])
        vts(nsink.ap, prep.ap[:, PR_SINK:PR_SINK + 16], -1.0, None, ALU.mult, r=[prep.b()], w=[nsink.b()])
        act(nA.ap[0:64, :], pcols.ap[0:64, PC_ALOG:PC_ALOG + 1], AF.Exp, r=[pcols.b()], w=[nA.b()])
        vts(nA.ap[0:64, :], nA.ap[0:64, :], -1.0, None, ALU.mult, r=[nA.b()], w=[nA.b()])

        wstate = {'next': 0, 'total': 0}
        blk_list = []
        for ti in range(nt):
            if 0 in layers:
                blk_list += list(range(0, NBLK0))
            if 1 in layers:
                blk_list += list(range(NBLK0, NBLK0 + NBLK1))
        wstate['total'] = len(blk_list)

        def w_issue():
            i = wstate['next']
            if i >= wstate['total']:
                return
            wstate['next'] += 1
            slot = i % NSLOT
            src = ws_d[blk_list[i]]
            dst = wsl[slot]
            sc.dma(POOL, lambda e: e.dma_start(out=dst.ap, in_=src), ds_w[slot], w=[dst.b()])

        wuse = {'i': 0}

        def w_next():
            i = wuse['i']
            wuse['i'] += 1
            while wstate['next'] < min(i + NSLOT, wstate['total']):
                w_issue()
            return wsl[i % NSLOT]

        def norm_stats(src_chunks, nfeat):
            sb_ = bank('A')
            n = len(src_chunks)
            for i, (ap, bf) in enumerate(src_chunks):
                s_ = sq[i % 2]
                act(s_.ap, ap, AF.Square, r=[bf], w=[s_.b()])
                mm(sb_.ap, ones_b.ap, s_.ap, start=(i == 0), stop=(i == n - 1),
                   r=[ones_b.b(), s_.b()], w=[sb_.b()])
            finish_stats(sb_, nfeat)

        def finish_stats(sb_, nfeat):
            act(lnt.ap, sb_.ap, AF.Ln, bias=EPS, scale=1.0 / nfeat, r=[sb_.b()], w=[lnt.b()])
            act(rstd.ap, lnt.ap, AF.Exp, scale=-0.5, r=[lnt.b()], w=[rstd.b()])

        def pre_norm(n):
            norm_stats([(hT.ap[:, f, :], hT.b(f)) for f in range(8)], DM)
            for f in range(8):
                vstt(uT.ap[:, f, :], hT.ap[:, f, :], pcols.ap[:, PC_NW + n * 8 + f:PC_NW + n * 8 + f + 1],
                     rstd.ap, ALU.mult, ALU.mult, r=[hT.b(f), pcols.b(), rstd.b()], w=[uT.b(f)])

        def post_norm(n, dst=None):
            dst = dst or hT
            for f in range(8):
                vtt(mT.ap[:, f, :], mT.ap[:, f, :], rstd.ap, ALU.mult, r=[mT.b(f), rstd.b()], w=[mT.b(f)])
                vstt(dst.ap[:, f, :], mT.ap[:, f, :], pcols.ap[:, PC_NW + n * 8 + f:PC_NW + n * 8 + f + 1],
                     hT.ap[:, f, :], ALU.mult, ALU.add, r=[mT.b(f), pcols.b(), hT.b(f)], w=[dst.b(f)])

        class OutAcc:
            def __init__(self, bias_col=None):
                self.sb_ = bank('B')
                self.pend = None
                self.i = 0
                self.bias_col = bias_col

            def chunk(self, oc, bk):
                if self.bias_col is None:
                    act(mT.ap[:, oc, :], bk.ap, AF.Copy, r=[bk.b()], w=[mT.b(oc)])
                else:
                    c0 = self.bias_col + oc
                    act(mT.ap[:, oc, :], bk.ap, AF.Identity, bias=pcols.ap[:, c0:c0 + 1],
                        r=[bk.b(), pcols.b()], w=[mT.b(oc)])
                s_ = sq[oc % 2]
                act(s_.ap, mT.ap[:, oc, :], AF.Square, r=[mT.b(oc)], w=[s_.b()])
                self.flush()
                self.pend = (oc, s_)

            def flush(self, last=False):
                if self.pend is not None:
                    oc, s_ = self.pend
                    mm(self.sb_.ap, ones_b.ap, s_.ap, start=(self.i == 0), stop=last,
                       r=[ones_b.b(), s_.b()], w=[self.sb_.b()])
                    self.i += 1
                    self.pend = None

            def finish(self):
                self.flush(last=True)
                finish_stats(self.sb_, DM)

        def mlp(n_pre, n_post, dst=None):
            pre_norm(n_pre)
            for b_ in range(8):
                sl = w_next()
                w3 = sl.ap.rearrange("p (k j) -> p k j", k=8)
                for j in range(4):
                    fc = 4 * b_ + j
                    bk = bank('A')
                    for k in range(8):
                        mm(bk.ap, w3[:, k, j * 128:(j + 1) * 128], uT.ap[:, k, :], start=(k == 0), stop=(k == 7),
                           r=[sl.b(), uT.b(k)], w=[bk.b()])
                    r_ = rl[fc % 2]
                    act(r_.ap, bk.ap, AF.Relu, r=[bk.b()], w=[r_.b()])
                    vtt(aT.ap[:, fc, :], r_.ap, bk.ap, ALU.mult, r=[r_.b(), bk.b()], w=[aT.b(fc)])
            oa = OutAcc()
            for oc in range(8):
                sl = w_next()
                w3 = sl.ap.rearrange("p (k j) -> p k j", k=32)
                bk = bank('A')
                for k in range(32):
                    mm(bk.ap, w3[:, k, :], aT.ap[:, k, :], start=(k == 0), stop=(k == 31),
                       r=[sl.b(), aT.b(k)], w=[bk.b()])
                oa.chunk(oc, bk)
            oa.finish()
            post_norm(n_post, dst)

        def ssd_layer(ti):
            pre_norm(0)
            chk(1)
            def ssd_A(g):
                par = g % 2
                for j in range(4):
                    col = PC_CONVW + (4 * g + j) * 4
                    vtt(dg.ap[:, 4 * j:4 * j + 4, :], ident_b.ap.unsqueeze(1).broadcast_to([128, 4, 128]),
                        pcols.ap[:, col:col + 4].unsqueeze(2).broadcast_to([128, 4, 128]), ALU.mult,
                        r=[ident_b.b(), pcols.b()], w=[dg.b(4 * j, 4 * j + 4)], E=POOL)
                DI = DIg[par]
                col = PR_D + 4 * g
                vtt(DI.ap, ident_b.ap.unsqueeze(1).broadcast_to([128, 4, 128]),
                    prep.ap[:, col:col + 4].unsqueeze(2).broadcast_to([128, 4, 128]), ALU.mult,
                    r=[ident_b.b(), prep.b()], w=[DI.b()], E=POOL)
                if g % 2 == 0:
                    sl = w_next()
                    w3 = sl.ap.rearrange("p (k j) -> p k j", k=8)
                    zc = zs[(g // 2) % 2]
                    for c in range(NCH):
                        bk = bank('A')
                        for k in range(8):
                            mm(bk.ap, uT.ap[:, k, c * 128:(c + 1) * 128], w3[:, k, :], start=(k == 0), stop=(k == 7),
                               r=[uT.b(k), sl.b()], w=[bk.b()])
                        act(zc.ap[:, c, :], bk.ap, AF.Silu, r=[bk.b()], w=[zc.b(c)])
                        yield
                sl = w_next()
                w3 = sl.ap.rearrange("p (k j) -> p k j", k=8)
                xr = xraw[par]
                xcg = xc[par]
                vcopy(xr.ap[:, :, 0:3], tails.ap[:, 4 * g:4 * g + 4, :], r=[tails.b(4 * g, 4 * g + 4)], w=[xr.b()])
                for j in range(4):
                    bk = bank('A')
                    for k in range(8):
                        mm(bk.ap, w3[:, k, j * 128:(j + 1) * 128], uT.ap[:, k, :], start=(k == 0), stop=(k == 7),
                           r=[sl.b(), uT.b(k)], w=[bk.b()])
                    if j == 2:
                        act(xr.ap[:, j, 3:3 + T], bk.ap, AF.Copy, r=[bk.b()], w=[xr.b(j)])
                    else:
                        vcopy(xr.ap[:, j, 3:3 + T], bk.ap, r=[bk.b()], w=[xr.b(j)])
                    yield
                vcopy(tails.ap[:, 4 * g:4 * g + 4, :], xr.ap[:, :, T:T + 3], r=[xr.b()], w=[tails.b(4 * g, 4 * g + 4)])
                yield
                for j in range(4):
                    bk = bank('A')
                    for k in range(4):
                        mm(bk.ap, dg.ap[:, 4 * j + k, :], xr.ap[:, j, k:k + T], start=(k == 0), stop=(k == 3),
                           r=[dg.b(4 * j + k), xr.b(j)], w=[bk.b()])
                    col = PC_CONVB + 4 * g + j
                    act(xcg.ap[:, j, :], bk.ap, AF.Silu, bias=pcols.ap[:, col:col + 1],
                        r=[bk.b(), pcols.b()], w=[xcg.b(j)])
                    yield
                if ti == 0 and g == 0:
                    tap("xc0", xcg.ap.rearrange("p a b -> p (a b)"), xcg.b())
                xbg = xbt[par]
                xwg = xw[par]
                for half in range(2):
                    tb = bank('A')
                    tbb = tb.ap.bitcast(BF16)
                    for cc in range(2):
                        c = 2 * half + cc
                        for j in range(3):
                            trp(tbb[:, cc * 384 + j * 128: cc * 384 + (j + 1) * 128], xcg.ap[:, j, c * 128:(c + 1) * 128],
                                ident_b.ap, r=[xcg.b(j), ident_b.b()], w=[tb.b()])
                    if half == 0:
                        act(xbg.ap[:, 2 * half:2 * half + 2, :], tbb[:, 0:768].rearrange("p (c f) -> p c f", c=2), AF.Copy,
                            r=[tb.b()], w=[xbg.b(2 * half, 2 * half + 2)])
                    else:
                        vcopy(xbg.ap[:, 2 * half:2 * half + 2, :], tbb[:, 0:768].rearrange("p (c f) -> p c f", c=2),
                              r=[tb.b()], w=[xbg.b(2 * half, 2 * half + 2)])
                    for cc in range(2):
                        c = 2 * half + cc
                        vtt(xwg.ap[:, c, :].rearrange("p (h d) -> p h d", h=4),
                            xbg.ap[:, c, 0:256].rearrange("p (h d) -> p h d", h=4),
                            tk[c].ap[:, 3, 4 * g:4 * g + 4].unsqueeze(2).broadcast_to([128, 4, 64]), ALU.mult,
                            r=[xbg.b(c), tk[c].b(3)], w=[xwg.b(c)], E=POOL)
                    yield
                Gb = bank('B')
                for c in range(NCH):
                    mm(Gb.ap[:, c * 128:(c + 1) * 128], xcg.ap[:, 2, c * 128:(c + 1) * 128],
                       xcg.ap[:, 3, c * 128:(c + 1) * 128], r=[xcg.b(2), xcg.b(3)], w=[Gb.b()])
                yield
                MT = MTb[par]
                for j in range(4):
                    h = 4 * g + j
                    Rb = bank('C')
                    for i in range(2):
                        mm(Rb.ap, ident_b.ap[32:64, 32 + h:33 + h].broadcast_to([32, 128]), csp.ap[32:64, i, :],
                           start=(i == 0), stop=False, r=[ident_b.b(), csp.b(i)], w=[Rb.b()])
                    mm(Rb.ap, ident_b.ap, mscan_b.ap, start=False, stop=True, r=[ident_b.b(), mscan_b.b()], w=[Rb.b()])
                    for c in range(NCH):
                        E_ = Ebuf[(j * NCH + c) % 4]
                        act(E_.ap, Rb.ap[:, c * 128:(c + 1) * 128], AF.Exp, bias=tk[c].ap[:, 1, h:h + 1],
                            r=[Rb.b(), tk[c].b(1)], w=[E_.b()])
                        vstt(MT.ap[:, j * NCH + c, :], Gb.ap[:, c * 128:(c + 1) * 128], tk[c].ap[:, 0, h:h + 1], E_.ap,
                             ALU.mult, ALU.mult, r=[Gb.b(), tk[c].b(0), E_.b()], w=[MT.b(j * NCH + c)])
                    yield
                if ti == 0 and g == 0:
                    tap("MT0", MT.ap.rearrange("p a b -> p (a b)"), MT.b())

            def ssd_B(g):
                par = g % 2
                zc = zs[(g // 2) % 2]
                zoff = (g % 2) * 256
                xcg, xbg, xwg, MT, DI, ygg = xc[par], xbt[par], xw[par], MTb[par], DIg[par], yg[par]
                for c in range(NCH):
                    yib = bank('A')
                    mm(yib.ap[:, 0:256], xcg.ap[:, 3, c * 128:(c + 1) * 128], S_bf.ap[:, g, :],
                       r=[xcg.b(3), S_bf.b(g)], w=[yib.b()])
                    yi_ = yi[c % 2]
                    vtt(yi_.ap.rearrange("p (h d) -> p h d", h=4), yib.ap[:, 0:256].rearrange("p (h d) -> p h d", h=4),
                        tk[c].ap[:, 2, 4 * g:4 * g + 4].unsqueeze(2).broadcast_to([128, 4, 64]), ALU.mult,
                        r=[yib.b(), tk[c].b(2)], w=[yi_.b()])
                    Snb = bank('A')
                    mm(Snb.ap[:, 0:256], xbg.ap[:, c, 256:384], xwg.ap[:, c, :], r=[xbg.b(c), xwg.b(c)], w=[Snb.b()])
                    vtt(S.ap[:, g, :].rearrange("p (h d) -> p h d", h=4),
                        S.ap[:, g, :].rearrange("p (h d) -> p h d", h=4),
                        tk[c].ap[:, 4, 4 * g:4 * g + 4].unsqueeze(2).broadcast_to([128, 4, 64]), ALU.mult,
                        r=[S.b(g), tk[c].b(4)], w=[S.b(g)])
                    vtt(S.ap[:, g, :], S.ap[:, g, :], Snb.ap[:, 0:256], ALU.add, r=[S.b(g), Snb.b()], w=[S.b(g)])
                    act(S_bf.ap[:, g, :], S.ap[:, g, :], AF.Copy, r=[S.b(g)], w=[S_bf.b(g)])
                    yield
                    yab = bank('A')
                    for j in range(4):
                        mm(yab.ap[:, j * 64:(j + 1) * 64], MT.ap[:, j * NCH + c, :], xbg.ap[:, c, j * 64:(j + 1) * 64],
                           start=True, stop=False, r=[MT.b(j * NCH + c), xbg.b(c)], w=[yab.b()])
                        mm(yab.ap[:, j * 64:(j + 1) * 64], DI.ap[:, j, :], xbg.ap[:, c, j * 64:(j + 1) * 64],
                           start=False, stop=True, r=[DI.b(j), xbg.b(c)], w=[yab.b()])
                    t2_ = t2[c % 2]
                    vtt(t2_.ap, yab.ap[:, 0:256], yi_.ap, ALU.add, r=[yab.b(), yi_.b()], w=[t2_.b()])
                    vtt(ygg.ap[:, c, :], t2_.ap, zc.ap[:, c, zoff:zoff + 256], ALU.mult,
                        r=[t2_.b(), zc.b(c)], w=[ygg.b(c)], E=POOL)
                    act(junk.ap, ygg.ap[:, c, :], AF.Square, accum=ssg.ap[:, par, c:c + 1],
                        r=[ygg.b(c)], w=[junk.b(), ssg.b(par)])
                    yield
                vts(rsg.ap[:, par, :], ssg.ap[:, par, :], 1.0 / 256.0, EPS, ALU.mult, ALU.add,
                    r=[ssg.b(par)], w=[rsg.b(par)], E=POOL)
                vtt(rsg.ap[:, par, :], rsg.ap[:, par, :], mhalf.ap, ALU.pow, r=[rsg.b(par), mhalf.b()], w=[rsg.b(par)], E=POOL)
                for c in range(NCH):
                    vtt(yn.ap[:, c, :], ygg.ap[:, c, :], rsg.ap[:, par, c:c + 1].broadcast_to([128, 256]), ALU.mult,
                        r=[ygg.b(c), rsg.b(par)], w=[yn.b(c)], E=POOL)
                yield 'defer'
                tb = bank('A')
                tbb = tb.ap.bitcast(BF16)
                for jj in range(2):
                    for c in range(NCH):
                        trp(tbb[:, jj * 512 + c * 128: jj * 512 + (c + 1) * 128], yn.ap[:, c, jj * 128:(jj + 1) * 128],
                            ident_b.ap, r=[yn.b(c), ident_b.b()], w=[tb.b()])
                for jj in range(2):
                    col = PC_SNW + 2 * g + jj
                    vts(ynT.ap[:, 2 * g + jj, :], tbb[:, jj * 512:(jj + 1) * 512], pcols.ap[:, col:col + 1], None, ALU.mult,
                        r=[tb.b(), pcols.b()], w=[ynT.b(2 * g + jj)])

            def drain(gen):
                for _ in gen:
                    pass

            bk = bank('A')
            for k in range(8):
                mm(bk.ap[0:64, :], wdt.ap[:, k, :], uT.ap[:, k, :], start=(k == 0), stop=(k == 7),
                   r=[wdt.b(), uT.b(k)], w=[bk.b()])
            act(dc.ap[0:64, :], bk.ap[0:64, :], AF.Exp, bias=pcols.ap[0:64, PC_DTB:PC_DTB + 1],
                r=[bk.b(), pcols.b()], w=[dc.b()])
            act(dc.ap[0:64, :], dc.ap[0:64, :], AF.Ln, bias=1.0, r=[dc.b()], w=[dc.b()])
            chk(21)
            vts(atmp.ap[32:64, :], dc.ap[32:64, :], nA.ap[32:64, 0:1], None, ALU.mult,
                r=[dc.b(), nA.b()], w=[atmp.b()])
            chk(22)
            sc.op(DVE, lambda e: e.tensor_tensor_scan(dc.ap[32:64, :], cst.ap[32:64, C_RESET:C_RESET + T],
                                                      atmp.ap[32:64, :], 0.0, ALU.mult, ALU.add),
                  r=[cst.b(), atmp.b()], w=[dc.b()])
            chk(2)
            vcopy(csp.ap[32:64, 0, :], dc.ap[32:64, :], r=[dc.b()], w=[csp.b(0)])
            vtt(cres.ap[32:64, :], dc.ap[32:64, :], csp.ap[32:64, 0, :], ALU.subtract, r=[dc.b(), csp.b(0)], w=[cres.b()])
            vcopy(csp.ap[32:64, 1, :], cres.ap[32:64, :], r=[cres.b()], w=[csp.b(1)])
            chk(23)
            A0 = ssd_A(0)
            for _ in range(6):
                next(A0)
            for c in range(NCH):
                last = c * 128 + 127
                vts(diagc.ap[32:64, c, :], cst.ap[32:64, C_ID + 32:C_ID + 64], dc.ap[32:64, last:last + 1], None,
                    ALU.mult, r=[cst.b(), dc.b()], w=[diagc.b(c)])
            clb = bank('A')
            mm(clb.ap[:, 0:128], ones_f.ap[32:64, :], diagc.ap[32:64, :, :].rearrange("p a b -> p (a b)"),
               r=[ones_f.b(), diagc.b()], w=[clb.b()])
            chk(24)
            vcopy(clS.ap, clb.ap[:, 0:128], r=[clb.b()], w=[clS.b()])
            for c in range(NCH):
                tb = bank('A')
                trp(tb.ap[:, 0:128], dc.ap[:, c * 128:(c + 1) * 128], cst.ap[:, C_ID:C_ID + 128],
                    r=[dc.b(), cst.b()], w=[tb.b()])
                tkc = tk[c]
                vcopy(tkc.ap[:, 0:2, :], tb.ap[:, 0:64].rearrange("p (a b) -> p a b", a=2), r=[tb.b()], w=[tkc.b(0, 2)])
                if c == 0:
                    chk(25)
                act(tkc.ap[:, 2, :], tkc.ap[:, 1, :], AF.Exp, r=[tkc.b(1)], w=[tkc.b(2)])
                vtt(tkt.ap, clS.ap[:, c * 32:(c + 1) * 32], tkc.ap[:, 1, :], ALU.subtract,
                    r=[clS.b(), tkc.b(1)], w=[tkt.b()])
                vts(tkc.ap[:, 1, :], tkc.ap[:, 1, :], -1.0, None, ALU.mult, r=[tkc.b(1)], w=[tkc.b(1)])
                if c == 0:
                    chk(26)
                act(tkt.ap, tkt.ap, AF.Exp, r=[tkt.b()], w=[tkt.b()])
                if c == 0:
                    chk(27)
                vtt(tkc.ap[:, 3, :], tkt.ap, tkc.ap[:, 0, :], ALU.mult, r=[tkt.b(), tkc.b(0)], w=[tkc.b(3)])
                act(tkc.ap[:, 4, :], clS.ap[:, c * 32:(c + 1) * 32], AF.Exp, r=[clS.b()], w=[tkc.b(4)])
                if c == 0:
                    chk(28)
                if c == 1:
                    chk(29)
            if ti == 0:
                tap("dc", dc.ap[0:64, :], dc.b())
                tap("tk0", tk[0].ap.rearrange("p a b -> p (a b)"), tk[0].b())

            chk(3)

            drain(A0)
            for g in range(8):
                gb = ssd_B(g)
                ga = ssd_A(g + 1) if g < 7 else iter(())
                done_a = done_b = False
                hold = 0
                while not (done_a and done_b):
                    if not done_a:
                        try:
                            next(ga)
                        except StopIteration:
                            done_a = True
                    if not done_b:
                        if hold > 0 and not done_a:
                            hold -= 1
                        else:
                            try:
                                if next(gb) == 'defer':
                                    hold = 3
                            except StopIteration:
                                done_b = True
            if ti == 0:
                tap("ynT", ynT.ap.rearrange("p a b -> p (a b)"), ynT.b())
            chk(9)
            oa = OutAcc()
            for b_ in range(4):
                sl = w_next()
                w3 = sl.ap.rearrange("p (k j) -> p k j", k=16)
                for o2 in range(2):
                    oc = 2 * b_ + o2
                    bk = bank('A')
                    for k in range(16):
                        mm(bk.ap, w3[:, k, o2 * 128:(o2 + 1) * 128], ynT.ap[:, k, :], start=(k == 0), stop=(k == 15),
                           r=[sl.b(), ynT.b(k)], w=[bk.b()])
                    oa.chunk(oc, bk)
            oa.finish()
            if ti == 0:
                tap("mT0", mT.ap.rearrange("p a b -> p (a b)"), mT.b())
            post_norm(1)
            if ti == 0:
                tap("h1", hT.ap.rearrange("p a b -> p (a b)"), hT.b())

        def attn_layer(ti):
            pre_norm(4)
            for b_ in range(2):
                sl = w_next()
                w3 = sl.ap.rearrange("p (k j) -> p k j", k=8)
                for j in range(4):
                    qc = 4 * b_ + j
                    bk = bank('A')
                    for k in range(8):
                        mm(bk.ap, w3[:, k, j * 128:(j + 1) * 128], uT.ap[:, k, :], start=(k == 0), stop=(k == 7),
                           r=[sl.b(), uT.b(k)], w=[bk.b()])
                    col = PC_QB + qc
                    act(qT.ap[:, qc, :], bk.ap, AF.Identity, bias=pcols.ap[:, col:col + 1],
                        r=[bk.b(), pcols.b()], w=[qT.b(qc)])
            sl = w_next()
            w3 = sl.ap.rearrange("p (k j) -> p k j", k=8)
            vcopy(kT.ap[:, :, 0:128], kprev.ap, r=[kprev.b()], w=[kT.b()])
            for j in range(4):
                bk = bank('A')
                for k in range(8):
                    mm(bk.ap, w3[:, k, j * 128:(j + 1) * 128], uT.ap[:, k, :], start=(k == 0), stop=(k == 7),
                       r=[sl.b(), uT.b(k)], w=[bk.b()])
                col = PC_KB + j
                act(kT.ap[:, j, 128:128 + T], bk.ap, AF.Identity, bias=pcols.ap[:, col:col + 1],
                    r=[bk.b(), pcols.b()], w=[kT.b(j)])
            vcopy(kprev.ap, kT.ap[:, :, T:T + 128], r=[kT.b()], w=[kprev.b()])
            sl = w_next()
            w3 = sl.ap.rearrange("p (k j) -> p k j", k=8)
            vcopy(Vt.ap[:, 0, :], vprev.ap, r=[vprev.b()], w=[Vt.b(0)])
            for c in range(NCH):
                bk = bank('A')
                for k in range(8):
                    mm(bk.ap[:, 0:256], uT.ap[:, k, c * 128:(c + 1) * 128], w3[:, k, 0:256], start=(k == 0), stop=(k == 7),
                       r=[uT.b(k), sl.b()], w=[bk.b()])
                vtt(Vt.ap[:, 1 + c, :], bk.ap[:, 0:256], prep.ap[:, PR_BV:PR_BV + 256], ALU.add,
                    r=[bk.b(), prep.b()], w=[Vt.b(1 + c)])
            vcopy(vprev.ap, Vt.ap[:, NCH, :], r=[Vt.b(NCH)], w=[vprev.b()])
            obs = {}
            Sbs = {}

            def st1(c, kv):
                gb = ti * NCH + c
                par = c % 2
                mo = 256 if gb == 0 else 0
                Sb = [bank('A'), bank('A')]
                Sbs[(c, kv)] = Sb
                for jp in range(2):
                    for j in (2 * jp, 2 * jp + 1):
                        h = 4 * kv + j
                        qc, pb = h // 2, 64 * (h % 2)
                        bj = Sb[j % 2]
                        c0 = (j // 2) * 256
                        mm(bj.ap[:, c0:c0 + 256], qT.ap[pb:pb + 64, qc, c * 128:(c + 1) * 128],
                           kT.ap[pb:pb + 64, kv, c * 128:c * 128 + 256],
                           start=True, stop=False, r=[qT.b(qc), kT.b(kv)], w=[bj.b()])
                    for j in (2 * jp, 2 * jp + 1):
                        bj = Sb[j % 2]
                        c0 = (j // 2) * 256
                        mm(bj.ap[:, c0:c0 + 256], ident_b.ap, matt_b.ap[:, mo:mo + 256], start=False, stop=True,
                           r=[ident_b.b(), matt_b.b()], w=[bj.b()])
                for i in range(2):
                    sc.op(DVE, lambda e, o=mr.ap[:, par, 4 * kv + i:4 * kv + i + 3:2],
                          i_=Sb[i].ap.rearrange("p (a b) -> p a b", a=2): e.reduce_max(o, i_, AX.X),
                          r=[Sb[i].b()], w=[mr.b(par)])
                vstt(negm.ap[:, par, 4 * kv:4 * kv + 4], mr.ap[:, par, 4 * kv:4 * kv + 4], -0.125,
                     nsink.ap[:, 4 * kv:4 * kv + 4], ALU.mult, ALU.min,
                     r=[mr.b(par), nsink.b()], w=[negm.b(par)])
                P_ = P4[kv % 2]
                for j in range(4):
                    h = 4 * kv + j
                    bj = Sb[j % 2]
                    c0 = (j // 2) * 256
                    act(P_.ap[:, j, :], bj.ap[:, c0:c0 + 256], AF.Exp, bias=negm.ap[:, par, h:h + 1], scale=0.125,
                        accum=rsum.ap[:, par, h:h + 1], r=[bj.b(), negm.b(par)], w=[P_.b(j), rsum.b(par)])

            def st2(c, kv):
                if kv == 0:
                    obs[c] = [bank('B'), bank('B')]
                ob = obs[c]
                P_ = P4[kv % 2]
                tb = bank('C')
                tbb = tb.ap.bitcast(BF16)
                for j in range(4):
                    for i in range(2):
                        trp(tbb[:, (2 * j + i) * 128:(2 * j + i + 1) * 128], P_.ap[:, j, i * 128:(i + 1) * 128], ident_b.ap,
                            r=[P_.b(j), ident_b.b()], w=[tb.b()])
                PT_ = PT4[kv % 2]
                vcopy(PT_.ap, tbb, r=[tb.b()], w=[PT_.b()])
                for j in range(4):
                    h = 4 * kv + j
                    o_ = ob[h // 8]
                    oc0 = (h % 8) * 64
                    mm(o_.ap[:, oc0:oc0 + 64], PT_.ap[:, (2 * j) * 128:(2 * j + 1) * 128], Vt.ap[:, c, kv * 64:(kv + 1) * 64],
                       start=True, stop=False, r=[PT_.b(), Vt.b(c)], w=[o_.b()])
                    mm(o_.ap[:, oc0:oc0 + 64], PT_.ap[:, (2 * j + 1) * 128:(2 * j + 2) * 128],
                       Vt.ap[:, c + 1, kv * 64:(kv + 1) * 64],
                       start=False, stop=True, r=[PT_.b(), Vt.b(c + 1)], w=[o_.b()])

            def st3(c):
                par = c % 2
                ob = obs[c]
                vtt(esk.ap[:, par, :], negm.ap[:, par, :], nsink.ap, ALU.subtract, r=[negm.b(par), nsink.b()], w=[esk.b(par)])
                act(esk.ap[:, par, :], esk.ap[:, par, :], AF.Exp, r=[esk.b(par)], w=[esk.b(par)])
                vtt(rden.ap[:, par, :], esk.ap[:, par, :], rsum.ap[:, par, :], ALU.add, r=[esk.b(par), rsum.b(par)], w=[rden.b(par)])
                sc.op(DVE, lambda e, o=rden.ap[:, par, :]: e.reciprocal(o, o), r=[rden.b(par)], w=[rden.b(par)])
                ao_ = ao[par]
                for i in range(2):
                    vtt(ao_.ap[:, i * 512:(i + 1) * 512].rearrange("p (h d) -> p h d", h=8),
                        ob[i].ap.rearrange("p (h d) -> p h d", h=8),
                        rden.ap[:, par, 8 * i:8 * i + 8].unsqueeze(2).broadcast_to([128, 8, 64]), ALU.mult,
                        r=[ob[i].b(), rden.b(par)], w=[ao_.b()])
                tb = bank('C')
                tbb = tb.ap.bitcast(BF16)
                for f in range(8):
                    trp(tbb[:, f * 128:(f + 1) * 128], ao_.ap[:, f * 128:(f + 1) * 128], ident_b.ap,
                        r=[ao_.b(), ident_b.b()], w=[tb.b()])
                act(aoT.ap[:, :, c * 128:(c + 1) * 128], tbb.rearrange("p (f t) -> p f t", f=8), AF.Copy,
                    r=[tb.b()], w=[aoT.b()])

            grps = [(c, kv) for c in range(NCH) for kv in range(4)]
            st1(*grps[0])
            for gi, (c, kv) in enumerate(grps):
                if gi + 1 < len(grps):
                    st1(*grps[gi + 1])
                st2(c, kv)
                if kv == 3:
                    st3(c)
            if ti == 0:
                tap("aoT", aoT.ap.rearrange("p a b -> p (a b)"), aoT.b())
            oa = OutAcc(bias_col=PC_OB)
            for b_ in range(2):
                sl = w_next()
                w3 = sl.ap.rearrange("p (k j) -> p k j", k=8)
                for j in range(4):
                    oc = 4 * b_ + j
                    bk = bank('A')
                    for k in range(8):
                        mm(bk.ap, w3[:, k, j * 128:(j + 1) * 128], aoT.ap[:, k, :], start=(k == 0), stop=(k == 7),
                           r=[sl.b(), aoT.b(k)], w=[bk.b()])
                    oa.chunk(oc, bk)
            oa.finish()
            post_norm(5)

        prefetched = set()

        def x_dma(ti, c):
            if (ti, c) in prefetched or ti >= nt:
                return
            prefetched.add((ti, c))
            xi = xin[c % 2]
            r0 = ti * T + c * 128
            sc.dma(SP, lambda e, xi=xi, r0=r0: e.dma_start(out=xi.ap, in_=x_d[r0:r0 + 128, :]), ds_xin[c % 2], w=[xi.b()])

        def load_tile(ti):
            for c in range(NCH):
                xi = xin[c % 2]
                x_dma(ti, c)
                for half in range(2):
                    tb = bank('A')
                    for i in range(4):
                        f = 4 * half + i
                        trp(tb.ap[:, i * 128:(i + 1) * 128], xi.ap[:, f * 128:(f + 1) * 128], cst.ap[:, C_ID:C_ID + 128],
                            r=[xi.b(), cst.b()], w=[tb.b()])
                    act(hT.ap[:, 4 * half:4 * half + 4, c * 128:(c + 1) * 128], tb.ap.rearrange("p (f t) -> p f t", f=4),
                        AF.Copy, r=[tb.b()], w=[hT.b(4 * half, 4 * half + 4)])

        def store_tile(ti, src):
            for c in range(NCH):
                xo = xout[c % 2]
                for half in range(2):
                    tb = bank('A')
                    for i in range(4):
                        f = 4 * half + i
                        trp(tb.ap[:, i * 128:(i + 1) * 128], src.ap[:, f, c * 128:(c + 1) * 128], cst.ap[:, C_ID:C_ID + 128],
                            r=[src.b(f), cst.b()], w=[tb.b()])
                    act(xo.ap[:, half * 512:(half + 1) * 512], tb.ap, AF.Copy, r=[tb.b()], w=[xo.b()])
                r0 = ti * T + c * 128
                tokn = sc.dma(SP, lambda e, xo=xo, r0=r0: e.dma_start(out=o_d[r0:r0 + 128, :], in_=xo.ap), ds_xout[c % 2], r=[xo.b()])
                out_tokens.append(tokn)

        last_layer = max(layers) if layers else None
        for ti in range(nt):
            if ti == 0:
                load_tile(ti)
            src = hT
            try:
                if 0 in layers:
                    ssd_layer(ti)
                    chk(10)
                    mlp(2, 3, dst=(mT if last_layer == 0 else None))
                    if last_layer == 0:
                        src = mT
                if 1 in layers:
                    attn_layer(ti)
                    x_dma(ti + 1, 0)
                    x_dma(ti + 1, 1)
                    mlp(6, 7, dst=mT)
                    src = mT
            except _Stop:
                pass
            if ti + 1 < nt and src is mT:
                load_tile(ti + 1)
                store_tile(ti, src)
            else:
                store_tile(ti, src)
                if ti + 1 < nt:
                    load_tile(ti + 1)
        sc.wait_tokens(SP, out_tokens)

        with nc.Block() as block:
            sc.replay(nc, block)
    return nc


def _wblock(W, cols, nk):
    sub = W[:, cols]
    a = sub.reshape(nk, 128, len(cols)).transpose(1, 0, 2)
    return np.ascontiguousarray(a).reshape(128, nk * len(cols))


def _xbc_cols(g):
    x0 = 2048 + g * 256
    b0 = 2048 + 2048 + g * 128
    c0 = 2048 + 3072 + g * 128
    return np.concatenate([np.arange(x0, x0 + 256), np.arange(b0, b0 + 128), np.arange(c0, c0 + 128)])


def host_prep(inp):
    f = np.float32
    w_in = np.asarray(inp["ssd_w_in"][0], f)
    ws = np.zeros((NBLK0 + NBLK1, 128, WBLK), f)
    bi = 0
    for b in range(4):
        ws[bi] = _wblock(w_in, np.arange(b * 512, (b + 1) * 512), 8); bi += 1
        for g in (2 * b, 2 * b + 1):
            ws[bi] = _wblock(w_in, _xbc_cols(g), 8); bi += 1
    w_out = np.asarray(inp["ssd_w_out"][0], f)
    for b in range(4):
        ws[bi] = _wblock(w_out, np.arange(b * 256, (b + 1) * 256), 16); bi += 1
    for layer in range(2):
        if layer == 1:
            wq = np.asarray(inp["attn_w_qkv"][0], f)
            for b in range(2):
                ws[bi] = _wblock(wq, np.arange(b * 512, (b + 1) * 512), 8); bi += 1
            kc = np.concatenate([1024 + jj * 64 + (np.arange(128) % 64) for jj in range(4)])
            ws[bi] = _wblock(wq, kc, 8); bi += 1
            ws[bi][:, :] = 0
            ws[bi].reshape(128, 8, 512)[:, :, 0:256] = _wblock(wq, np.arange(1280, 1536), 8).reshape(128, 8, 256); bi += 1
            wo = np.asarray(inp["attn_w_o"][0], f)
            for b in range(2):
                ws[bi] = _wblock(wo, np.arange(b * 512, (b + 1) * 512), 8); bi += 1
        wu = np.asarray(inp["mlp_w_up"][layer], f)
        wd = np.asarray(inp["mlp_w_down"][layer], f)
        for b in range(8):
            ws[bi] = _wblock(wu, np.arange(b * 512, (b + 1) * 512), 8); bi += 1
        for oc in range(8):
            ws[bi] = _wblock(wd, np.arange(oc * 128, (oc + 1) * 128), 32); bi += 1
    assert bi == NBLK0 + NBLK1
    dtc = 6144 + (np.arange(64) % 32)
    wdt = _wblock(w_in, dtc, 8)
    pc = np.zeros((128, NPC), f)
    norms = [inp["mix_pre_norm"][0], inp["mix_post_norm"][0], inp["ffn_pre_norm"][0], inp["ffn_post_norm"][0],
             inp["mix_pre_norm"][1], inp["mix_post_norm"][1], inp["ffn_pre_norm"][1], inp["ffn_post_norm"][1]]
    for n, wv in enumerate(norms):
        pc[:, PC_NW + n * 8:PC_NW + n * 8 + 8] = np.asarray(wv, f).reshape(8, 128).T
    cb = np.asarray(inp["ssd_conv_b"][0], f)
    cw = np.asarray(inp["ssd_conv_w"][0], f)
    for g in range(8):
        cols = _xbc_cols(g) - 2048
        for j in range(4):
            ch = cols[j * 128:(j + 1) * 128]
            pc[:, PC_CONVB + 4 * g + j] = cb[ch]
            for k in range(4):
                pc[:, PC_CONVW + (4 * g + j) * 4 + k] = cw[k, ch]
    bq = np.asarray(inp["attn_b_qkv"][0], f)
    pc[:, PC_QB:PC_QB + 8] = bq[0:1024].reshape(8, 128).T
    for jj in range(4):
        pc[:, PC_KB + jj] = bq[1024 + jj * 64 + (np.arange(128) % 64)]
    pc[:, PC_OB:PC_OB + 8] = np.asarray(inp["attn_b_o"][0], f).reshape(8, 128).T
    pc[:, PC_DTB] = np.asarray(inp["ssd_dt_bias"][0], f)[np.arange(128) % 32]
    pc[:, PC_ALOG] = np.asarray(inp["ssd_a_log"][0], f)[np.arange(128) % 32]
    pc[:, PC_SNW:PC_SNW + 16] = np.asarray(inp["ssd_norm_w"][0], f).reshape(16, 128).T
    pr = np.zeros((128, NPR), f)
    pr[:, PR_D:PR_D + 32] = np.asarray(inp["ssd_d"][0], f)[None, :]
    pr[:, PR_BV:PR_BV + 256] = bq[1280:1536][None, :]
    pr[:, PR_SINK:PR_SINK + 16] = np.asarray(inp["attn_sinks"][0], f)[None, :]
    cs = np.zeros((128, NCST), f)
    cs[:, C_ID:C_ID + 128] = np.eye(128, dtype=f)
    s_ = np.arange(128)[:, None]
    t_ = np.arange(128)[None, :]
    m = np.where(t_ >= s_, 0.0, NEG).astype(f)
    mk = np.zeros((128, 1024), f)
    mk[:, 0:512] = np.tile(m, (1, 4))
    q_ = np.arange(128)[:, None]
    k_ = np.arange(128)[None, :]
    prev = np.where(k_ > q_, 0.0, NEG).astype(f)
    cur = np.where(k_ <= q_, 0.0, NEG).astype(f)
    mk[:, 512:640] = prev
    mk[:, 640:768] = cur
    mk[:, 768:896] = NEG
    mk[:, 896:1024] = cur
    rm = np.ones((T,), f)
    rm[0::128] = 0.0
    cs[:, C_RESET:C_RESET + T] = rm[None, :]
    return {"wstream": ws, "wdt": wdt, "pcols": pc, "prep": pr, "cst": cs, "msk": mk}


_CACHE = {}


def _get_nc(layers):
    key = tuple(layers)
    if key not in _CACHE:
        _CACHE[key] = build(layers=key)
    return _CACHE[key]


FUSED = True


def kernel(**inputs):
    x = np.asarray(inputs["x"], np.float32)
    hp = host_prep(inputs)
    B = x.shape[0]
    cur = [np.ascontiguousarray(x[b]) for b in range(B)]
    plans = [(0, 1)] if FUSED else [(0,), (1,)]
    for layers in plans:
        nc = _get_nc(layers)
        in_maps = [dict(hp, x=cur[b]) for b in range(B)]
        res = run_bass_kernel_spmd(nc, in_maps, core_ids=list(range(B)))
        cur = [np.asarray(res.results[b]["out"], np.float32) for b in range(B)]
    return np.stack(cur, axis=0).astype(np.float32)
```
